# Optimizing a Trainium2 kernel written in Bass

```python
import math
import jax, jax.numpy as jnp
from jax import lax
import numpy as np

D_MODEL = 2048
BATCH = 2
SEQ = 4096
DEPTH = 2

HEAD_DIM = D_MODEL // 16
POOL_WINDOWS = (2, 4, 8, 16)
POOL_GROUP = HEAD_DIM
POOL_WIDTH = len(POOL_WINDOWS) * POOL_GROUP
DIL_PAIRS = ((128, 1), (512, 4), (2048, 16))
DIL_HEADS_PER_GROUP = 4
DIL_HEADS = DIL_HEADS_PER_GROUP * len(DIL_PAIRS)
DIL_WIDTH = DIL_HEADS * HEAD_DIM
DIL_OUT = DIL_HEADS_PER_GROUP * HEAD_DIM
GRID_W = 64
NA_ROWS_MAX = 8
NA_COLS = 16
NA_HEADS = 8
NA_WIDTH = NA_HEADS * HEAD_DIM
SG_CHUNK = 128
SG_GROUPS = 8
SG_WIDTH = 1024
D_FF = 5632
N_EVEN = (DEPTH + 1) // 2
N_ODD = DEPTH // 2
EVEN_IN = POOL_WIDTH + 3 * DIL_WIDTH
EVEN_OUT = POOL_WIDTH + DIL_OUT
ODD_IN = 3 * NA_WIDTH + 2 * SG_WIDTH
ODD_OUT = NA_WIDTH + SG_WIDTH
RMS_EPS = 1e-6
NEG_INF = -1e30

kernel_name = "hybrid_pool_dilated_natten_gmlp_encoder"


def rms_norm(x, g):
    xf = x.astype(jnp.float32)
    y = xf * lax.rsqrt(jnp.mean(xf * xf, axis=-1, keepdims=True) + RMS_EPS)
    return (y * g.astype(jnp.float32)).astype(x.dtype)


def alibi_slopes(n):
    return jnp.asarray(2.0 ** (-8.0 * np.arange(1, n + 1) / n), dtype=jnp.float32)


def swiglu(h, w_gate, w_up, w_down):
    return (jax.nn.silu(h @ w_gate) * (h @ w_up)) @ w_down


def pool_mixer(a, w_pool, pool_scale):
    Bsz, S, _ = a.shape
    af = a.reshape(Bsz, S, len(POOL_WINDOWS), POOL_GROUP).astype(jnp.float32)
    cs = jnp.concatenate([jnp.zeros_like(af[:, :1]), jnp.cumsum(af, axis=1)], axis=1)
    t = jnp.arange(S)
    outs = []
    for g, w in enumerate(POOL_WINDOWS):
        lo = jnp.clip(t - w // 2, 0, S)
        hi = jnp.clip(t + w // 2, 0, S)
        cs_g = cs[:, :, g]
        cnt = (hi - lo).astype(jnp.float32)[None, :, None]
        outs.append((cs_g[:, hi] - cs_g[:, lo]) / cnt - af[:, :, g])
    pooled = jnp.stack(outs, axis=2).astype(a.dtype)
    y = jnp.einsum('bsgc,gce->bsge', pooled, w_pool)
    return y.reshape(Bsz, S, POOL_WIDTH) * pool_scale


def dilated_group_attn(q, k, v, window, dil, slopes):
    Bsz, H, S, Dh = q.shape
    half = window // (2 * dil)
    blk = half
    L = S // dil
    nb = -(-L // blk)
    Lp = nb * blk

    def to_res(x):
        return x.reshape(Bsz, H, L, dil, Dh).transpose(0, 1, 3, 2, 4)

    qb = jnp.pad(to_res(q), ((0, 0), (0, 0), (0, 0), (0, Lp - L), (0, 0))).reshape(Bsz, H, dil, nb, blk, Dh)

    def halo(x):
        xp = jnp.pad(x, ((0, 0), (0, 0), (0, 0), (blk, Lp - L + blk), (0, 0))).reshape(Bsz, H, dil, nb + 2, blk, Dh)
        return jnp.concatenate([xp[:, :, :, :nb], xp[:, :, :, 1:nb + 1], xp[:, :, :, 2:nb + 2]], axis=4)

    kb, vb = halo(to_res(k)), halo(to_res(v))
    s = jnp.einsum('bhrnid,bhrnjd->bhrnij', qb, kb, preferred_element_type=jnp.float32)
    i = jnp.arange(blk)
    j = jnp.arange(3 * blk)
    n = jnp.arange(nb)
    rel = j[None, :] - blk - i[:, None]
    key_pos = (n[:, None] - 1) * blk + j[None, :]
    valid = (jnp.abs(rel) <= half)[None] & ((key_pos >= 0) & (key_pos < L))[:, None, :]
    alibi = -slopes[:, None, None] * (jnp.abs(rel) * dil).astype(jnp.float32)[None]
    s = jnp.where(valid[None, None, None], s + alibi[None, :, None, None], NEG_INF)
    m = jnp.max(s, axis=-1)
    p = jnp.exp(s - m[..., None])
    l = jnp.sum(p, axis=-1)
    o = jnp.einsum('bhrnij,bhrnjd->bhrnid', p, vb.astype(jnp.float32)) / l[..., None]

    def from_res(x):
        x = x.reshape((Bsz, H, dil, Lp) + x.shape[5:])[:, :, :, :L]
        x = jnp.moveaxis(x, 2, 3)
        return x.reshape((Bsz, H, S) + x.shape[4:])

    return from_res(o), from_res(m), from_res(l)


def dilated_mixer(q, k, v, q_gain, k_gain):
    Bsz, S, _ = q.shape

    def heads(x):
        return x.reshape(Bsz, S, DIL_HEADS, HEAD_DIM).transpose(0, 2, 1, 3)

    qh = rms_norm(heads(q), q_gain) * (HEAD_DIM ** -0.5)
    kh = rms_norm(heads(k), k_gain)
    vh = heads(v)
    slopes = alibi_slopes(DIL_HEADS)
    outs, maxs, dens = [], [], []
    for g, (w, d) in enumerate(DIL_PAIRS):
        sl = slice(g * DIL_HEADS_PER_GROUP, (g + 1) * DIL_HEADS_PER_GROUP)
        o, m, l = dilated_group_attn(qh[:, sl], kh[:, sl], vh[:, sl], w, d, slopes[sl])
        outs.append(o)
        maxs.append(m)
        dens.append(l)
    O = jnp.stack(outs)
    M = jnp.stack(maxs)
    Lden = jnp.stack(dens)
    wts = Lden * jnp.exp(M - jnp.max(M, axis=0, keepdims=True))
    out = jnp.sum(wts[..., None] * O, axis=0) / jnp.sum(wts, axis=0)[..., None]
    return out.transpose(0, 2, 1, 3).reshape(Bsz, S, DIL_OUT).astype(q.dtype)


def neighbourhood_mixer(q, k, v, q_gain, k_gain, rpb):
    Bsz, S, _ = q.shape
    rows = S // GRID_W
    wr = min(NA_ROWS_MAX, rows)

    def grid(x):
        return x.reshape(Bsz, rows, GRID_W, NA_HEADS, HEAD_DIM).transpose(0, 3, 1, 2, 4)

    qg = rms_norm(grid(q), q_gain) * (HEAD_DIM ** -0.5)
    kg = rms_norm(grid(k), k_gain)
    vg = grid(v)
    r = jnp.arange(rows)
    row_start = jnp.clip(r - wr // 2, 0, rows - wr)
    row_idx = row_start[:, None] + jnp.arange(wr)[None, :]
    k_rows = kg[:, :, row_idx]
    v_rows = vg[:, :, row_idx]
    c = jnp.arange(GRID_W)
    col_start = jnp.clip(c - NA_COLS // 2, 0, GRID_W - NA_COLS)
    col_ok = (c[None, :] >= col_start[:, None]) & (c[None, :] < col_start[:, None] + NA_COLS)
    rel_r = row_idx - r[:, None] + (NA_ROWS_MAX - 1)
    rel_c = jnp.clip(c[None, :] - c[:, None], -(NA_COLS - 1), NA_COLS - 1) + (NA_COLS - 1)
    bias = rpb[:, rel_r[:, None, :, None], rel_c[None, :, None, :]].astype(jnp.float32)
    s = jnp.einsum('bhrqd,bhrwkd->bhrqwk', qg, k_rows, preferred_element_type=jnp.float32) + bias[None]
    s = jnp.where(col_ok[:, None, :], s, NEG_INF)
    p = jax.nn.softmax(s.reshape(Bsz, NA_HEADS, rows, GRID_W, wr * GRID_W), axis=-1).reshape(s.shape)
    o = jnp.einsum('bhrqwk,bhrwkd->bhrqd', p, v_rows.astype(jnp.float32))
    return o.transpose(0, 2, 3, 1, 4).reshape(Bsz, S, NA_WIDTH).astype(q.dtype)


def spatial_gating(u, v, v_gain, w_s, b_s):
    Bsz, S, _ = u.shape
    nc = S // SG_CHUNK
    vn = rms_norm(v, v_gain).reshape(Bsz, nc, SG_CHUNK, SG_GROUPS, SG_WIDTH // SG_GROUPS)
    sv = jnp.einsum('gpq,bnqgc->bnpgc', w_s, vn) + b_s.T[None, None, :, :, None]
    return u * sv.reshape(Bsz, S, SG_WIDTH)


def setup_inputs(seed: int = 0) -> dict:
    key = jax.random.key(seed)
    ks = jax.random.split(key, 24)
    f32 = jnp.float32

    def nrm(k, shape, scale):
        return jax.random.normal(k, shape, f32) * scale

    def gain(k, shape):
        return 1.0 + 0.02 * jax.random.normal(k, shape, f32)

    return {
        "x": jax.random.normal(ks[0], (BATCH, SEQ, D_MODEL), f32),
        "norm_ffn1": gain(ks[1], (DEPTH, D_MODEL)),
        "norm_mix": gain(ks[2], (DEPTH, D_MODEL)),
        "norm_ffn2": gain(ks[3], (DEPTH, D_MODEL)),
        "norm_out": gain(ks[4], (DEPTH, D_MODEL)),
        "ffn_w_gate": nrm(ks[5], (DEPTH, 2, D_MODEL, D_FF), D_MODEL ** -0.5),
        "ffn_w_up": nrm(ks[6], (DEPTH, 2, D_MODEL, D_FF), D_MODEL ** -0.5),
        "ffn_w_down": nrm(ks[7], (DEPTH, 2, D_FF, D_MODEL), D_FF ** -0.5),
        "even_w_in": nrm(ks[8], (N_EVEN, D_MODEL, EVEN_IN), D_MODEL ** -0.5),
        "pool_w": nrm(ks[9], (N_EVEN, len(POOL_WINDOWS), POOL_GROUP, POOL_GROUP), POOL_GROUP ** -0.5),
        "pool_scale": gain(ks[10], (N_EVEN, POOL_WIDTH)),
        "dil_q_gain": gain(ks[11], (N_EVEN, HEAD_DIM)),
        "dil_k_gain": gain(ks[12], (N_EVEN, HEAD_DIM)),
        "even_w_out": nrm(ks[13], (N_EVEN, EVEN_OUT, D_MODEL), EVEN_OUT ** -0.5),
        "odd_w_in": nrm(ks[14], (N_ODD, D_MODEL, ODD_IN), D_MODEL ** -0.5),
        "na_q_gain": gain(ks[15], (N_ODD, HEAD_DIM)),
        "na_k_gain": gain(ks[16], (N_ODD, HEAD_DIM)),
        "na_rpb": nrm(ks[17], (N_ODD, NA_HEADS, 2 * NA_ROWS_MAX - 1, 2 * NA_COLS - 1), 0.1),
        "sg_v_gain": gain(ks[18], (N_ODD, SG_WIDTH)),
        "sg_w": nrm(ks[19], (N_ODD, SG_GROUPS, SG_CHUNK, SG_CHUNK), SG_CHUNK ** -0.5),
        "sg_b": gain(ks[20], (N_ODD, SG_GROUPS, SG_CHUNK)),
        "odd_w_out": nrm(ks[21], (N_ODD, ODD_OUT, D_MODEL), ODD_OUT ** -0.5),
    }


def reference(x, norm_ffn1, norm_mix, norm_ffn2, norm_out, ffn_w_gate, ffn_w_up, ffn_w_down,
              even_w_in, pool_w, pool_scale, dil_q_gain, dil_k_gain, even_w_out,
              odd_w_in, na_q_gain, na_k_gain, na_rpb, sg_v_gain, sg_w, sg_b, odd_w_out):
    for layer in range(DEPTH):
        h = rms_norm(x, norm_ffn1[layer])
        x = x + 0.5 * swiglu(h, ffn_w_gate[layer, 0], ffn_w_up[layer, 0], ffn_w_down[layer, 0])
        h = rms_norm(x, norm_mix[layer])
        if layer % 2 == 0:
            e = layer // 2
            z = h @ even_w_in[e]
            a, qb, kb, vb = jnp.split(z, [POOL_WIDTH, POOL_WIDTH + DIL_WIDTH, POOL_WIDTH + 2 * DIL_WIDTH], axis=-1)
            ya = pool_mixer(a, pool_w[e], pool_scale[e])
            yb = dilated_mixer(qb, kb, vb, dil_q_gain[e], dil_k_gain[e])
            y = jnp.concatenate([ya, yb], axis=-1) @ even_w_out[e]
        else:
            o = layer // 2
            z = h @ odd_w_in[o]
            qc, kc, vc, uv = jnp.split(z, [NA_WIDTH, 2 * NA_WIDTH, 3 * NA_WIDTH], axis=-1)
            yc = neighbourhood_mixer(qc, kc, vc, na_q_gain[o], na_k_gain[o], na_rpb[o])
            u, v = jnp.split(jax.nn.gelu(uv), 2, axis=-1)
            yd = spatial_gating(u, v, sg_v_gain[o], sg_w[o], sg_b[o])
            y = jnp.concatenate([yc, yd], axis=-1) @ odd_w_out[o]
        x = x + y
        h = rms_norm(x, norm_ffn2[layer])
        x = x + 0.5 * swiglu(h, ffn_w_gate[layer, 1], ffn_w_up[layer, 1], ffn_w_down[layer, 1])
        x = rms_norm(x, norm_out[layer])
    return x
```

```python
from contextlib import ExitStack
import numpy as np
import ml_dtypes
import concourse.bass as bass
import concourse.mybir as mybir
from concourse.bass import ds
from concourse.bass_utils import run_bass_kernel_spmd

F32 = mybir.dt.float32
BF16 = mybir.dt.bfloat16
I32 = mybir.dt.int32
AF = mybir.ActivationFunctionType
ALU = mybir.AluOpType

NT = 1024
D = 2048
DFF = 5632
NRE = 6160
NRO = 4096
NSE = 3344
NSO = 2048
EPS = 1e-6
NWB = 4


class S:
    def __init__(self, sem):
        self.sem = sem
        self.n = 0


class Q:
    def __init__(self, name):
        self.name = name
        self.prog = []
        self.s = None
        self.waited = {}

    def wait(self, ev):
        if ev is None:
            return
        s, val = ev
        if self.waited.get(id(s), 0) >= val:
            return
        if s is self.s and val <= s.n - 4:
            return
        self.waited[id(s)] = val
        self.prog.append(lambda e, s=s, val=val: e.wait_ge(s.sem, val))

    def op(self, fn, waits=(), sig=True):
        for ev in waits:
            self.wait(ev)
        if sig:
            s = self.s
            s.n += 1
            self.prog.append(lambda e, fn=fn, s=s: fn(e).then_inc(s.sem, 1))
            return (s, s.n)
        self.prog.append(lambda e, fn=fn: fn(e))
        return None

    def dma(self, out, in_, ds_, waits=()):
        for ev in waits:
            self.wait(ev)
        ds_.n += 16
        self.prog.append(lambda e, out=out, in_=in_, ds_=ds_: e.dma_start(
            out=out, in_=(in_() if callable(in_) else in_)).then_inc(ds_.sem, 16))
        return (ds_, ds_.n)


def sl(start, count, step):
    return slice(start, start + step * (count - 1) + 1, step)


class Ring:
    def __init__(self, items):
        self.items = items
        self.free = [None] * len(items)
        self.i = 0

    def get(self):
        k = self.i % len(self.items)
        self.i += 1
        return k, self.items[k], self.free[k]

    def rel(self, k, ev):
        self.free[k] = ev


def build():
    nc = bass.Bass("TRN2", target_bir_lowering=False)

    def din(name, shape, dt=F32):
        return nc.dram_tensor(name, list(shape), dt, kind="ExternalInput").ap()

    xT_d = din("xT", [D, NT])
    gains_d = din("gains", [128, 128])
    hg_d = din("hg", [128, 4])
    pscale_d = din("pscale", [128, 4])
    wg_d = din("wg", [4, D, DFF])
    wu_d = din("wu", [4, D, DFF])
    wd_d = din("wd", [4, DFF, D])
    ewin_d = din("ewin", [D, 5120])
    ewout_d = din("ewout", [1024, D])
    owin_d = din("owin", [D, 5120])
    owout_d = din("owout", [D, D])
    poolw_d = din("poolw", [4, 128, 128])
    sgw_d = din("sgw", [8, 128, 128])
    sgb_d = din("sgb", [128, 1024])
    sgvg_d = din("sgvg", [128, 1024])
    etab_d = din("etab", [128, 12 * 256], BF16)
    vcols_d = din("vcols", [128, 4])
    natab_d = din("natab", [128, 128, 512])
    poolfix_d = din("poolfix", [128, 64])
    pvalid_d = din("pvalid", [128, 2])
    info_d = din("info", [1, 4], I32)
    out_d = nc.dram_tensor("outT", [D, NT], F32, kind="ExternalOutput").ap()
    slabE = nc.dram_tensor("slabE", [NRE, 512], BF16)
    sendE = nc.dram_tensor("sendE", [NSE, 512], BF16)
    gathE = nc.dram_tensor("gathE", [8 * NSE, 512], BF16)
    slabO = nc.dram_tensor("slabO", [NRO, 512], BF16)
    sendO = nc.dram_tensor("sendO", [NSO, 512], BF16)
    gathO = nc.dram_tensor("gathO", [8 * NSO, 512], BF16)
    qscr = nc.dram_tensor("qscr", [12 * 128, NT], BF16)
    nbr = {"prevE": nc.dram_tensor("nb_prevE", [NSE, 512], BF16), "nextE": nc.dram_tensor("nb_nextE", [NSE, 512], BF16),
           "prevO": nc.dram_tensor("nb_prevO", [NSO, 512], BF16), "nextO": nc.dram_tensor("nb_nextO", [NSO, 512], BF16)}

    PE, ACT, DVE, POOL, SP = Q("tensor"), Q("scalar"), Q("vector"), Q("gpsimd"), Q("sync")
    engs = [PE, ACT, DVE, POOL, SP]

    with ExitStack() as st:
        def sem(name):
            return S(st.enter_context(nc.semaphore(name)))

        uid = [0]

        def sb(name, shape, dt, stack=None):
            uid[0] += 1
            return (stack or st).enter_context(nc.sbuf_tensor("s%d_%s" % (uid[0], name), list(shape), dt))

        for q in engs:
            q.s = sem("c_" + q.name)
        cc_sem = sem("cc")

        pb = [st.enter_context(nc.psum_tensor("pb%d" % i, [128, 512], F32)) for i in range(8)]

        xT = sb("xT", [128, 16, NT], F32)
        hT = sb("hT", [128, 16, NT], BF16)
        ones = sb("ones", [128, 128], BF16)
        gains = sb("gains_s", [128, 128], F32)
        hg = sb("hg_s", [128, 6], F32)
        pscale = sb("pscale_s", [128, 4], F32)
        vcols = sb("vcols_s", [128, 4], F32)
        pvalid = sb("pvalid_s", [128, 2], F32)
        poolfix = sb("poolfix_s", [128, 4, 16], F32)
        rstd = sb("rstd", [128, NT], F32)
        sq = [sb("sq%d" % i, [128, NT], BF16) for i in range(2)]
        wslots = [sb("w%d" % i, [128, 4096], BF16) for i in range(NWB)]
        wsem = [sem("wsem%d" % i) for i in range(NWB)]
        wring = Ring(list(range(NWB)))
        ld = sem("ld")
        ld_tab = sem("ld_tab")
        ld_pw = sem("ld_pw")
        ld_h = sem("ld_h")
        ld_c = [sem("ld_c%d" % i) for i in range(3)]
        stq = sem("st")

        bar_evs = []
        dyn = {}
        def barrier():
            evs = [(q.s, q.s.n) for q in engs if q.s.n > 0]
            for q in engs:
                if q is POOL:
                    continue
                for ev in evs:
                    if ev[0] is not q.s:
                        q.wait(ev)
            bar_evs[:] = evs

        def wload(src, shape3):
            k, _, free = wring.get()
            a, b = shape3
            view = wslots[k][:, 0:a * b].rearrange("p (a b) -> p a b", b=b)
            ev = POOL.dma(view, src, wsem[k], waits=[free])
            return view, ev, k

        def mm(out, lhsT, rhs, start, stop, waits=(), sig=False):
            return PE.op(lambda e: e.matmul(out, lhsT=lhsT, rhs=rhs, start=start, stop=stop), waits=waits, sig=sig)

        def act(out, in_, func, waits=(), **kw):
            return ACT.op(lambda e: e.activation(out=out, in_=in_, func=func, **kw), waits=waits)

        def tt(out, in0, in1, op, waits=()):
            return DVE.op(lambda e: e.tensor_tensor(out=out, in0=in0, in1=in1, op=op), waits=waits)

        def stt(out, in0, scalar, in1, op0, op1, waits=()):
            return DVE.op(lambda e: e.scalar_tensor_tensor(out=out, in0=in0, scalar=scalar, in1=in1, op0=op0, op1=op1), waits=waits)

        def ts(out, in0, s1, s2, op0, op1=None, waits=()):
            if op1 is None:
                return DVE.op(lambda e: e.tensor_scalar(out=out, in0=in0, scalar1=s1, scalar2=None, op0=op0), waits=waits)
            return DVE.op(lambda e: e.tensor_scalar(out=out, in0=in0, scalar1=s1, scalar2=s2, op0=op0, op1=op1), waits=waits)

        def recip(out, in_, waits=()):
            return DVE.op(lambda e: e.reciprocal(out=out, in_=in_), waits=waits)

        evs0 = []
        for c in range(16):
            evs0.append(SP.dma(xT[:, c, :], xT_d[c * 128:(c + 1) * 128, :], ld))
        for dst, src in ((gains[:], gains_d), (hg[:, 0:4], hg_d), (pscale[:], pscale_d), (vcols[:], vcols_d),
                         (pvalid[:], pvalid_d), (poolfix[:], poolfix_d.rearrange("p (a b) -> p a b", b=16))):
            evs0.append(SP.dma(dst, src, ld))
        ev_ld0 = evs0[-1]
        DVE.op(lambda e: e.memset(ones[:], 1.0))
        ts(hg[:, 4:5], hg[:, 0:1], 128.0 ** -0.5, None, ALU.mult, waits=[ev_ld0])
        ts(hg[:, 5:6], hg[:, 2:3], 128.0 ** -0.5, None, ALU.mult)
        prevE, nextE, prevO, nextO = "prevE", "nextE", "prevO", "nextO"
        barrier()

        sq_free = [None, None]

        def rms_norm_x(gidx, final):
            last = None
            for c in range(16):
                k = c % 2
                e_sq = act(sq[k][:], xT[:, c, :], AF.Square, waits=[sq_free[k]])
                for t in range(2):
                    last = mm(pb[t][:], ones[:], sq[k][:, t * 512:(t + 1) * 512], c == 0, c == 15,
                              waits=[e_sq], sig=(t == 1))
                sq_free[k] = last
            e1 = None
            for t in range(2):
                e1 = act(rstd[:, t * 512:(t + 1) * 512], pb[t][:], AF.Sqrt, waits=[last], bias=EPS, scale=1.0 / D)
            e2 = recip(rstd[:], rstd[:], waits=[e1])
            for c in range(16):
                dst = xT[:, c, :] if final else hT[:, c, :]
                stt(dst, xT[:, c, :], gains[:, gidx * 16 + c:gidx * 16 + c + 1], rstd[:], ALU.mult, ALU.mult, waits=[e2])
            barrier()

        def ffn(li):
            with ExitStack() as ph:
                actT = [sb("actT%d" % i, [128, 2, NT], BF16, ph) for i in range(2)]
                stmp = [sb("stmp%d" % i, [128, 512], F32, ph) for i in range(2)]
                uring = Ring([(pb[0], pb[1]), (pb[2], pb[3])])
                dring = Ring([pb[4], pb[5], pb[6], pb[7]])
                st_free = [None, None]
                NG = DFF // 256
                wgv = wg_d[li].rearrange("(c p) f -> p c f", p=128)
                wuv = wu_d[li].rearrange("(c p) f -> p c f", p=128)
                wdv = wd_d[li].rearrange("(c p) f -> p c f", p=128)
                act_ready = [None] * NG
                act_read = [None, None]
                pend = None

                def down(g, wdblk, wd_ev):
                    last = None
                    for oc in range(16):
                        for t in range(2):
                            k, bank, free = dring.get()
                            for j in range(2):
                                last = mm(bank[:], wdblk[:, j, oc * 128:(oc + 1) * 128],
                                          actT[g % 2][:, j, t * 512:(t + 1) * 512], j == 0, j == 1,
                                          waits=[free, wd_ev, act_ready[g]], sig=(j == 1))
                            ev = stt(xT[:, oc, t * 512:(t + 1) * 512], bank[:], 0.5, xT[:, oc, t * 512:(t + 1) * 512],
                                     ALU.mult, ALU.add, waits=[last])
                            dring.rel(k, ev)
                    return last

                for g in range(NG):
                    gblk, g_ev, gk = wload(wgv[:, :, g * 256:(g + 1) * 256], (16, 256))
                    ublk, u_ev, uk = wload(wuv[:, :, g * 256:(g + 1) * 256], (16, 256))
                    lastu = None
                    for fc in range(2):
                        for t in range(2):
                            k, (bg, bu), free = uring.get()
                            for c in range(16):
                                mm(bg[:], gblk[:, c, fc * 128:(fc + 1) * 128], hT[:, c, t * 512:(t + 1) * 512],
                                   c == 0, c == 15, waits=[free, g_ev, act_read[g % 2]])
                            for c in range(16):
                                lastu = mm(bu[:], ublk[:, c, fc * 128:(fc + 1) * 128], hT[:, c, t * 512:(t + 1) * 512],
                                           c == 0, c == 15, waits=[u_ev], sig=(c == 15))
                            sk = (fc * 2 + t) % 2
                            e_s = act(stmp[sk][:], bg[:], AF.Silu, waits=[lastu, st_free[sk]])
                            e_m = tt(actT[g % 2][:, fc, t * 512:(t + 1) * 512], stmp[sk][:], bu[:], ALU.mult, waits=[e_s])
                            st_free[sk] = e_m
                            uring.rel(k, e_m)
                            act_ready[g] = e_m
                    wring.rel(gk, lastu)
                    wring.rel(uk, lastu)
                    if pend is not None:
                        dblk, d_ev, dk = wload(wdv[:, 2 * pend:2 * pend + 2, :], (2, 2048))
                        l = down(pend, dblk, d_ev)
                        wring.rel(dk, l)
                        act_read[pend % 2] = l
                    pend = g
                dblk, d_ev, dk = wload(wdv[:, 2 * pend:2 * pend + 2, :], (2, 2048))
                l = down(pend, dblk, d_ev)
                wring.rel(dk, l)
                barrier()

        def head_units(win_d, col0, nheads, gcol, tmps, emit_out):
            qf, sqh, rt = tmps
            wv = win_d.rearrange("(c p) f -> p c f", p=128)
            mring = Ring([pb[0], pb[1], pb[2], pb[3]])
            sring = Ring([pb[4], pb[5]])
            free_t = [None, None]
            pending = []

            def finish(item):
                h, t, sk, e_sq, e_cp = item
                k2, sbank, free2 = sring.get()
                l2 = mm(sbank[:], ones[:], sqh[sk][:], True, True, waits=[e_sq, free2], sig=True)
                e_r = act(rt[sk][:], sbank[:], AF.Sqrt, waits=[l2], bias=EPS, scale=1.0 / 128)
                sring.rel(k2, e_r)
                e_r2 = recip(rt[sk][:], rt[sk][:], waits=[e_r])
                ev = emit_out(h, t, qf[sk], rt[sk], [e_r2, e_cp])
                free_t[sk] = ev

            ui = 0
            for b in range(nheads // 2):
                blk, b_ev, bk = wload(wv[:, :, col0 + b * 256: col0 + (b + 1) * 256], (16, 256))
                last = None
                for hh in range(2):
                    h = b * 2 + hh
                    for t in range(2):
                        k, bank, free = mring.get()
                        for c in range(16):
                            last = mm(bank[:], blk[:, c, hh * 128:(hh + 1) * 128], hT[:, c, t * 512:(t + 1) * 512],
                                      c == 0, c == 15, waits=[free, b_ev], sig=(c == 15))
                        sk = ui % 2
                        ui += 1
                        e_cp = act(qf[sk][:], bank[:], AF.Copy, waits=[last, free_t[sk]])
                        e_sq = act(sqh[sk][:], bank[:], AF.Square)
                        mring.rel(k, e_sq)
                        pending.append((h, t, sk, e_sq, e_cp))
                        if len(pending) > 1:
                            finish(pending.pop(0))
                wring.rel(bk, last)
            while pending:
                finish(pending.pop(0))

        def proj_fm(w_view, kch, col0, nblk, inT, consume):
            mring = Ring([pb[0], pb[1], pb[2], pb[3]])
            for b in range(nblk):
                blk, b_ev, bk = wload(w_view[:, :, col0 + b * 256: col0 + (b + 1) * 256], (kch, 256))
                last = None
                for hh in range(2):
                    for t in range(2):
                        k, bank, free = mring.get()
                        for c in range(kch):
                            last = mm(bank[:], blk[:, c, hh * 128:(hh + 1) * 128], inT[:, c, t * 512:(t + 1) * 512],
                                      c == 0, c == kch - 1, waits=[free, b_ev], sig=(c == kch - 1))
                        ev = consume(b * 2 + hh, t, bank, last)
                        mring.rel(k, ev)
                wring.rel(bk, last)

        def proj_tm(w_view, col0, nblk, consume):
            mring = Ring([pb[4], pb[5], pb[6], pb[7]])
            for b in range(nblk):
                blk, b_ev, bk = wload(w_view[:, :, col0 + b * 256: col0 + (b + 1) * 256], (16, 256))
                last = None
                for tb in range(8):
                    k, bank, free = mring.get()
                    for c in range(16):
                        last = mm(bank[:, 0:256], hT[:, c, tb * 128:(tb + 1) * 128], blk[:, c, :],
                                  c == 0, c == 15, waits=[free, b_ev], sig=(c == 15))
                    ev = consume(b, tb, bank, last)
                    mring.rel(k, ev)
                wring.rel(bk, last)

        def out_proj(w_d, kch, catT):
            wv = w_d.rearrange("(c p) f -> p c f", p=128)

            def cons(oc, t, bank, last):
                return tt(xT[:, oc, t * 512:(t + 1) * 512], bank[:], xT[:, oc, t * 512:(t + 1) * 512], ALU.add, waits=[last])
            proj_fm(wv, kch, 0, 8, catT, cons)
            barrier()

        def allgather(slab, gath, evs):
            for ev in evs:
                POOL.wait(ev)
            cc_sem.n += 1
            n = cc_sem.n
            POOL.prog.append(lambda e: e.collective_compute(
                "AllGather", ALU.bypass, replica_groups=[list(range(8))],
                ins=[slab.ap().opt()], outs=[gath.ap().opt()]).then_inc(cc_sem.sem, 1))
            return (cc_sem, n)

        def cK3(t, i): return t[256 * i:256 * i + 256, :].rearrange("(p two) c -> p (two c)", two=2)
        def cV3(t, i): return t[1024 + 256 * i:1024 + 256 * i + 256, :].rearrange("r (q f) -> (r q) f", f=128)
        def cK2(t, i): return t[2048 + 128 * i:2048 + 128 * i + 128, :]
        def cV2(t, i): return t[2560 + 128 * i:2560 + 128 * i + 128, :].rearrange("r (q f) -> (r q) f", f=128)
        def cK1(t, i): return t[3072 + 32 * i:3072 + 32 * i + 32, :].rearrange("r (q f) -> (r q) f", f=128)
        def cV1(t, i): return t[3200 + 32 * i:3200 + 32 * i + 32, :].rearrange("r (q f) -> (r q) f", f=128)
        def cA(t): return t[3328:3344, :].rearrange("r (g k) -> (r g) k", k=16).rearrange("(c p) k -> p c k", p=128)
        def oK(t, h): return t[128 * h:128 * h + 128, :]
        def oV(t, h): return t[1024 + 128 * h:1024 + 128 * h + 128, :].rearrange("r (q f) -> (r q) f", f=128)

        nb_sem = sem("nb")

        def fetch_nbrs(gath, nrows, keys, ev_cc):
            SP.wait(ev_cc)
            ev = None
            for key in keys:
                ev = SP.dma(nbr[key][:, :].rearrange("(a r) c -> a (r c)", a=16),
                            lambda key=key: gath[ds(dyn[key], nrows), :].rearrange("(a r) c -> a (r c)", a=16), nb_sem)
            return ev

        def even_mixer():
            with ExitStack() as mx:
                aT = sb("aT", [128, 4, NT + 16], F32, mx)
                catT = sb("catT", [128, 8, NT], BF16, mx)
                wv = ewin_d.rearrange("(c p) f -> p c f", p=128)
                slab_evs = []
                with ExitStack() as ph:
                    qf = [sb("qf%d" % i, [128, 512], F32, ph) for i in range(2)]
                    sqh = [sb("sqh%d" % i, [128, 512], BF16, ph) for i in range(2)]
                    rt = [sb("rt%d" % i, [128, 512], F32, ph) for i in range(2)]
                    htmp = [sb("htmp%d" % i, [128, NT], BF16, ph) for i in range(2)]
                    vst = [sb("vst%d" % i, [128, 8, 256], BF16, ph) for i in range(2)]
                    aedge = sb("aedge", [128, 4, 16], BF16, ph)
                    ht_free = [None, None]
                    hsem = [sem("hsemA"), sem("hsemB")]

                    a_evs = []

                    def cons_a(oc, t, bank, last):
                        a_evs.append(act(aT[:, oc, 8 + t * 512: 8 + (t + 1) * 512], bank[:], AF.Copy, waits=[last]))
                        return a_evs[-1]
                    proj_fm(wv, 16, 0, 2, hT, cons_a)
                    barrier()
                    ACT.prog.append(lambda e, s=ACT.s, v=a_evs[-1][1]: e.wait_ge(s.sem, v))
                    e1 = act(aedge[:, :, 0:8], aT[:, :, 8:16], AF.Copy)
                    e2 = act(aedge[:, :, 8:16], aT[:, :, NT:NT + 8], AF.Copy)
                    aview = cA(sendE)
                    slab_evs.append(SP.dma(aview, aedge[:], stq, waits=[e1, e2]))

                    q_evs = []

                    def mk_emit(gcol, dst_of_head, evlist, extra=None):
                        def emit(h, t, src, rs, waits):
                            slot = h % 2
                            w = list(waits)
                            if t == 0:
                                w.append(ht_free[slot])
                            ev = stt(htmp[slot][:, t * 512:(t + 1) * 512], src[:], hg[:, gcol:gcol + 1], rs[:],
                                     ALU.mult, ALU.mult, waits=w)
                            if t == 1:
                                dev = SP.dma(dst_of_head(h), htmp[slot][:], hsem[slot], waits=[ev])
                                if extra is not None:
                                    for d_, s_ in extra(h, htmp[slot]):
                                        dev = SP.dma(d_, s_, hsem[slot], waits=[ev])
                                ht_free[slot] = dev
                                evlist.append(dev)
                            return ev
                        return emit

                    def k_extra(h, t_):
                        if h < 4:
                            return [(cK1(sendE, h)[:, 0:64], t_[:, 0:64]), (cK1(sendE, h)[:, 64:128], t_[:, 960:1024])]
                        if h < 8:
                            return [(cK2(sendE, h - 4)[:, 0:256], t_[:, 0:256]), (cK2(sendE, h - 4)[:, 256:512], t_[:, 768:1024])]
                        return [(cK3(sendE, h - 8), t_[:])]
                    head_units(ewin_d, 2048, 12, 1, (qf, sqh, rt), mk_emit(
                        1, lambda h: slabE[256 * h:256 * h + 256, :].rearrange("(p two) c -> p (two c)", two=2), slab_evs, k_extra))
                    barrier()

                    vs_free = [None, None]
                    vsem = [sem("vsemA"), sem("vsemB")]

                    def cons_v(b, tb, bank, last):
                        w = [last]
                        if tb == 0:
                            w.append(vs_free[b % 2])
                        ev = act(vst[b % 2][:, tb, :], bank[:, 0:256], AF.Copy, waits=w)
                        if tb == 7:
                            for j in range(2):
                                h = 2 * b + j
                                dst = slabE[3072 + 256 * h:3072 + 256 * h + 256, :].rearrange(
                                    "r (q f) -> (r q) f", f=128).rearrange("(tb p) f -> p tb f", p=128)
                                vs_ = vst[b % 2]
                                jc = slice(j * 128, (j + 1) * 128)
                                dev = SP.dma(dst, vs_[:, :, jc], vsem[b % 2], waits=[ev])
                                if h < 4:
                                    SP.dma(cV1(sendE, h)[0:64, :], vs_[0:64, 0, jc], vsem[b % 2])
                                    dev = SP.dma(cV1(sendE, h)[64:128, :], vs_[64:128, 7, jc], vsem[b % 2])
                                elif h < 8:
                                    c2 = cV2(sendE, h - 4).rearrange("(tb p) f -> p tb f", p=128)
                                    SP.dma(c2[:, 0:2, :], vs_[:, 0:2, jc], vsem[b % 2])
                                    dev = SP.dma(c2[:, 2:4, :], vs_[:, 6:8, jc], vsem[b % 2])
                                else:
                                    dev = SP.dma(cV3(sendE, h - 8).rearrange("(tb p) f -> p tb f", p=128), vs_[:, :, jc], vsem[b % 2])
                                slab_evs.append(dev)
                            vs_free[b % 2] = dev
                        return ev
                    proj_tm(wv, 3584, 6, cons_v)
                    barrier()
                    ev_cc = allgather(sendE, gathE, slab_evs)
                    head_units(ewin_d, 512, 12, 4, (qf, sqh, rt), mk_emit(4, lambda h: qscr[h * 128:(h + 1) * 128, :], q_evs))
                    barrier()
                for ev in q_evs:
                    SP.wait(ev)

                with ExitStack() as ph:
                    etab = sb("etab", [128, 12, 256], BF16, ph)
                    pooled = [sb("pooled%d" % i, [128, NT], BF16, ph) for i in range(2)]
                    s_a = sb("s_a", [128, NT + 16], F32, ph)
                    s_b = sb("s_b", [128, NT + 16], F32, ph)
                    ahalo = sb("ahalo", [128, 4, 16], BF16, ph)
                    pexp = [sb("pexp%d" % i, [128, 512], F32, ph) for i in range(2)]
                    pT = [sb("pT%d" % i, [128, 512], BF16, ph) for i in range(3)]
                    rden = sb("rden", [128, 512], F32, ph)
                    pw = sb("pw", [128, 4, 128], BF16, ph)
                    hflat = hT[:].rearrange("p a b -> p (a b)")
                    o = 0

                    def carve(n):
                        nonlocal o
                        v = hflat[:, o:o + n]
                        o += n
                        return v
                    KT1 = carve(1152)
                    KT2 = carve(1536)
                    KT3 = carve(3072)
                    V1 = carve(9 * 128).rearrange("p (a b) -> p a b", b=128)
                    V2 = carve(12 * 128).rearrange("p (a b) -> p a b", b=128)
                    V3 = carve(32 * 128).rearrange("p (a b) -> p a b", b=128)
                    QT = carve(3 * NT).rearrange("p (a b) -> p a b", b=NT)
                    kvs = sem("kvs")

                    e_tab = SP.dma(etab[:], etab_d.rearrange("p (a b) -> p a b", b=256), ld_tab)
                    e_pw = POOL.dma(pw[:], poolw_d.rearrange("g c e -> c g e"), ld_pw, waits=list(bar_evs))
                    ev_nb = fetch_nbrs(gathE, NSE, (prevE, nextE), ev_cc)
                    SP.wait(ev_nb)

                    def av(key):
                        return cA(nbr[key])
                    eh1 = SP.dma(ahalo[:, :, 0:8], lambda: av(prevE)[:, :, 8:16], ld_h)
                    eh2 = SP.dma(ahalo[:, :, 8:16], lambda: av(nextE)[:, :, 0:8], ld_h)
                    ts(aT[:, :, 0:8], ahalo[:, :, 0:8], pvalid[:, 0:1], None, ALU.mult, waits=[eh1, eh2])
                    e_h = ts(aT[:, :, NT + 8:NT + 16], ahalo[:, :, 8:16], pvalid[:, 1:2], None, ALU.mult)
                    pring = Ring([pb[0], pb[1]])
                    po_free = [None, None]
                    for g, w in enumerate((2, 4, 8, 16)):
                        L = NT + 16
                        src = aT[:, g, :]
                        cur, nxt = s_a, s_b
                        step = 1
                        n = L
                        first = True
                        e_p = e_h
                        while step < w:
                            n = n - step
                            a_in = src if first else cur
                            e_p = tt(nxt[:, 0:n], a_in[:, 0:n], a_in[:, step:step + n], ALU.add, waits=[e_p])
                            cur, nxt = nxt, cur
                            first = False
                            step *= 2
                        o0 = 8 - w // 2
                        e_f1 = tt(cur[:, o0:o0 + 8], cur[:, o0:o0 + 8], poolfix[:, g, 0:8], ALU.mult, waits=[e_p])
                        e_f2 = tt(cur[:, o0 + NT - 8:o0 + NT], cur[:, o0 + NT - 8:o0 + NT], poolfix[:, g, 8:16], ALU.mult, waits=[e_f1])
                        e_po = stt(pooled[g % 2][:], cur[:, o0:o0 + NT], 1.0 / w, aT[:, g, 8:8 + NT], ALU.mult, ALU.subtract,
                                   waits=[e_f2, po_free[g % 2]])
                        last = None
                        for t in range(2):
                            k, bank, free = pring.get()
                            last = mm(bank[:], pw[:, g, :], pooled[g % 2][:, t * 512:(t + 1) * 512], True, True,
                                      waits=[free, e_po, e_pw], sig=True)
                            ev = act(catT[:, g, t * 512:(t + 1) * 512], bank[:], AF.Identity, waits=[last], scale=pscale[:, g:g + 1])
                            pring.rel(k, ev)
                        po_free[g % 2] = last
                    barrier()

                    def kview(base, h):
                        return nbr[base][256 * h:256 * h + 256, :].rearrange("(p two) c -> p (two c)", two=2)

                    def kown(h):
                        return slabE[256 * h:256 * h + 256, :].rearrange("(p two) c -> p (two c)", two=2)

                    def vview(base, h):
                        return nbr[base][3072 + 256 * h:3072 + 256 * h + 256, :].rearrange("r (q f) -> (r q) f", f=128)

                    def vown(h):
                        return slabE[3072 + 256 * h:3072 + 256 * h + 256, :].rearrange("r (q f) -> (r q) f", f=128)

                    sring = Ring([pb[0], pb[1], pb[2]])
                    oring = Ring([(pb[4], pb[5]), (pb[6], pb[7])])
                    pe_free = [None, None]
                    pt_free = [None, None, None]
                    cnt = [0, 0]
                    kv_read = None
                    pend = []
                    bi = [0]
                    ofree_ev = [None, None]
                    lastp_box = [None]

                    def stage2(it):
                        kp, n, nres, sbank = it["kp"], it["n"], it["nres"], it["sbank"]
                        tot = nres * n
                        ie = cnt[0] % 2
                        cnt[0] += 1
                        e_e = act(pexp[ie][0:kp, 0:tot], sbank[0:kp, 0:tot], AF.Exp, waits=[it["last"], pe_free[ie]])
                        sring.rel(it["ks"], e_e)
                        ip = cnt[1] % 3
                        cnt[1] += 1
                        if nres == 1:
                            tabv = it["tab"]
                            pin = pexp[ie][0:kp, 0:tot]
                            pout = pT[ip][0:kp, 0:tot]
                        else:
                            tabv = it["tab"].unsqueeze(1).broadcast_to([kp, nres, n])
                            pin = pexp[ie][0:kp, 0:tot].rearrange("p (r n) -> p r n", n=n)
                            pout = pT[ip][0:kp, 0:tot].rearrange("p (r n) -> p r n", n=n)
                        e_p = stt(pout, pin, vcols[0:kp, it["vcol"]:it["vcol"] + 1], tabv, ALU.mult, ALU.mult,
                                  waits=[e_e, pt_free[ip]])
                        pe_free[ie] = e_p
                        blk = it["blk"]
                        lastp = None
                        for r in range(nres):
                            rhs = pT[ip][0:kp, r * n:(r + 1) * n]
                            mm(it["oview"][r], it["vl"][r], rhs, blk["first"], False, waits=[e_p, ofree_ev[blk["pair"]]])
                            lastp = mm(it["dview"][r], ones[0:kp, :], rhs, blk["first"], False, sig=(r == nres - 1))
                            blk["first"] = False
                        pt_free[ip] = lastp
                        lastp_box[0] = lastp
                        if it["post"] is not None:
                            it["post"](lastp)
                    for hm in range(4):
                        ld_evs = []
                        w0 = [kv_read]
                        h1, h2, h3 = hm, 4 + hm, 8 + hm

                        def L(out, in_, w0=w0):
                            ld_evs.append(SP.dma(out, in_, kvs, waits=w0))
                        L(KT1[:, 64:1088], kown(h1))
                        L(KT1[:, 0:64], cK1(nbr[prevE], hm)[:, 64:128])
                        L(KT1[:, 1088:1152], cK1(nbr[nextE], hm)[:, 0:64])
                        L(KT2[:, 256:1280], kown(h2))
                        L(KT2[:, 0:256], cK2(nbr[prevE], hm)[:, 256:512])
                        L(KT2[:, 1280:1536], cK2(nbr[nextE], hm)[:, 0:256])
                        L(KT3[:, 1024:2048], kown(h3))
                        L(KT3[:, 0:1024], cK3(nbr[prevE], hm))
                        L(KT3[:, 2048:3072], cK3(nbr[nextE], hm))
                        for i, h in enumerate((h1, h2, h3)):
                            L(QT[:, i, :], qscr[h * 128:(h + 1) * 128, :])
                        vo = vown(h1)
                        L(V1[64:128, 0, :], vo[0:64, :])
                        L(V1[:, 1:8, :], vo[64:960, :].rearrange("(j p) f -> p j f", p=128))
                        L(V1[0:64, 8, :], vo[960:1024, :])
                        L(V1[0:64, 0, :], cV1(nbr[prevE], hm)[64:128, :])
                        L(V1[64:128, 8, :], cV1(nbr[nextE], hm)[0:64, :])
                        vo = vown(h2)
                        V2v = V2.rearrange("p (r j) f -> p r j f", j=3)
                        vo4 = vo.rearrange("(u r) f -> u r f", r=4)
                        L(V2v[64:128, :, 0, :], vo4[0:64])
                        L(V2v[:, :, 1, :], vo4[64:192])
                        L(V2v[0:64, :, 2, :], vo4[192:256])
                        L(V2v[0:64, :, 0, :], cV2(nbr[prevE], hm)[256:512, :].rearrange("(u r) f -> u r f", r=4))
                        L(V2v[64:128, :, 2, :], cV2(nbr[nextE], hm)[0:256, :].rearrange("(u r) f -> u r f", r=4))
                        vo = vown(h3)
                        V3v = V3.rearrange("p (r j) f -> p r j f", j=2)
                        L(V3v[0:64, :, 0, :], cV3(nbr[prevE], hm).rearrange("(u r) f -> u r f", r=16))
                        L(V3v[64:128, :, 0, :], vo.rearrange("(u r) f -> u r f", r=16))
                        L(V3v[0:64, :, 1, :], cV3(nbr[nextE], hm).rearrange("(u r) f -> u r f", r=16))
                        ev_kv = ld_evs[-1]

                        for qb in range(2):
                            Q0 = 512 * qb
                            pair = bi[0] % 2
                            bi[0] += 1
                            Ob, Db = oring.items[pair]
                            blk = {"first": True, "pair": pair}

                            def tile(klhs, qrhs, kp, n, tab, vcol, vl, oview, dview, nres=1, blk=blk):
                                ks, sbank, sfree = sring.get()
                                last = None
                                for r in range(nres):
                                    last = mm(sbank[0:kp, r * n:(r + 1) * n], klhs[r], qrhs[r], True, True,
                                              waits=[sfree, ev_kv, e_tab], sig=(r == nres - 1))
                                item = dict(ks=ks, sbank=sbank, last=last, kp=kp, n=n, nres=nres, tab=tab, vcol=vcol,
                                            vl=vl, oview=oview, dview=dview, blk=blk, post=None)
                                pend.append(item)
                                if len(pend) > 2:
                                    stage2(pend.pop(0))
                                return item

                            lastp = None
                            for j in range(4 * qb, 4 * qb + 5):
                                qlo = max(128 * (j - 1), Q0)
                                qhi = min(128 * (j + 1), Q0 + 512)
                                n = qhi - qlo
                                i0 = qlo - 128 * (j - 1)
                                vcol = 1 if j == 0 else (2 if j == 8 else 0)
                                it_ = tile([KT1[:, 128 * j:128 * j + 128]], [QT[:, 0, qlo:qhi]], 128, n,
                                             etab[:, hm, i0:i0 + n], vcol, [V1[:, j, :]],
                                             [Ob[:, qlo - Q0:qhi - Q0]], [Db[:, qlo - Q0:qhi - Q0]])
                            for (j, i0) in (((0, 128), (1, 0)) if qb == 0 else ((1, 128), (2, 0))):
                                u0 = 128 * qb
                                vcol = 1 if j == 0 else (2 if j == 2 else 0)
                                kl = [KT2[:, sl(512 * j + r, 128, 4)] for r in range(4)]
                                qr = [QT[:, 1, sl(4 * u0 + r, 128, 4)] for r in range(4)]
                                vl = [V2[:, r * 3 + j, :] for r in range(4)]
                                ov = [Ob[:, sl(r, 128, 4)] for r in range(4)]
                                dv = [Db[:, sl(r, 128, 4)] for r in range(4)]
                                it_ = tile(kl, qr, 128, 128, etab[:, 4 + hm, i0:i0 + 128], vcol, vl, ov, dv, nres=4)
                            uq0 = 32 * qb
                            qr = [QT[:, 2, sl(Q0 + r, 32, 16)] for r in range(16)]
                            ov = [Ob[:, sl(r, 32, 16)] for r in range(16)]
                            dv = [Db[:, sl(r, 32, 16)] for r in range(16)]
                            kl = [KT3[:, sl(r, 128, 16)] for r in range(16)]
                            vl = [V3[:, r * 2 + 0, :] for r in range(16)]
                            it_ = tile(kl, qr, 128, 32, etab[:, 8 + hm, 128 + uq0:128 + uq0 + 32], 1, vl, ov, dv, nres=16)
                            kl = [KT3[:, sl(2048 + r, 64, 16)] for r in range(16)]
                            vl = [V3[0:64, r * 2 + 1, :] for r in range(16)]
                            it_ = tile(kl, qr, 64, 32, etab[0:64, 8 + hm, uq0:uq0 + 32], 3, vl, ov, dv, nres=16)

                            def fin(lastp, Ob=Ob, Db=Db, hm=hm, Q0=Q0, pair=pair):
                                e_r = recip(rden[:], Db[:], waits=[lastp])
                                ofree_ev[pair] = tt(catT[:, 4 + hm, Q0:Q0 + 512], Ob[:], rden[:], ALU.mult, waits=[e_r])
                            it_["post"] = fin
                        while pend:
                            stage2(pend.pop(0))
                        kv_read = lastp_box[0]
                    barrier()
                out_proj(ewout_d, 8, catT)

        def odd_mixer():
            with ExitStack() as mx:
                catT = sb("catTo", [128, 16, NT], BF16, mx)
                wv = owin_d.rearrange("(c p) f -> p c f", p=128)
                slab_evs = []
                with ExitStack() as ph:
                    qf = [sb("oqf%d" % i, [128, 512], F32, ph) for i in range(2)]
                    sqh = [sb("osqh%d" % i, [128, 512], BF16, ph) for i in range(2)]
                    rt = [sb("ort%d" % i, [128, 512], F32, ph) for i in range(2)]
                    htmp = [sb("ohtmp%d" % i, [128, NT], BF16, ph) for i in range(2)]
                    vst = [sb("ovst%d" % i, [128, 8, 256], BF16, ph) for i in range(2)]
                    ht_free = [None, None]
                    hsem = [sem("ohsemA"), sem("ohsemB")]

                    q_evs = []

                    def mk_emit(gcol, dst_of_head, evlist, extra=None):
                        def emit(h, t, src, rs, waits):
                            slot = h % 2
                            w = list(waits)
                            if t == 0:
                                w.append(ht_free[slot])
                            ev = stt(htmp[slot][:, t * 512:(t + 1) * 512], src[:], hg[:, gcol:gcol + 1], rs[:],
                                     ALU.mult, ALU.mult, waits=w)
                            if t == 1:
                                dev = SP.dma(dst_of_head(h), htmp[slot][:], hsem[slot], waits=[ev])
                                if extra is not None:
                                    for d_, s_ in extra(h, htmp[slot]):
                                        dev = SP.dma(d_, s_, hsem[slot], waits=[ev])
                                ht_free[slot] = dev
                                evlist.append(dev)
                            return ev
                        return emit
                    head_units(owin_d, 1024, 8, 3, (qf, sqh, rt), mk_emit(
                        3, lambda h: slabO[256 * h:256 * h + 256, :].rearrange("(p two) c -> p (two c)", two=2), slab_evs,
                        lambda h, t_: [(oK(sendO, h)[:, 0:256], t_[:, 0:256]), (oK(sendO, h)[:, 256:512], t_[:, 768:1024])]))
                    barrier()

                    vs_free = [None, None]
                    vsem = [sem("ovsemA"), sem("ovsemB")]

                    def cons_v(b, tb, bank, last):
                        w = [last]
                        if tb == 0:
                            w.append(vs_free[b % 2])
                        ev = act(vst[b % 2][:, tb, :], bank[:, 0:256], AF.Copy, waits=w)
                        if tb == 7:
                            for j in range(2):
                                h = 2 * b + j
                                dst = slabO[2048 + 256 * h:2048 + 256 * h + 256, :].rearrange(
                                    "r (q f) -> (r q) f", f=128).rearrange("(tb p) f -> p tb f", p=128)
                                vs_ = vst[b % 2]
                                jc = slice(j * 128, (j + 1) * 128)
                                SP.dma(dst, vs_[:, :, jc], vsem[b % 2], waits=[ev])
                                c2 = oV(sendO, h).rearrange("(tb p) f -> p tb f", p=128)
                                SP.dma(c2[:, 0:2, :], vs_[:, 0:2, jc], vsem[b % 2])
                                dev = SP.dma(c2[:, 2:4, :], vs_[:, 6:8, jc], vsem[b % 2])
                                slab_evs.append(dev)
                            vs_free[b % 2] = dev
                        return ev
                    proj_tm(wv, 2048, 4, cons_v)
                    barrier()
                    ev_cc = allgather(sendO, gathO, slab_evs)
                    head_units(owin_d, 0, 8, 5, (qf, sqh, rt), mk_emit(5, lambda h: qscr[h * 128:(h + 1) * 128, :], q_evs))
                    barrier()

                with ExitStack() as ph:
                    VN = sb("VN", [128, 8, NT], BF16, ph)
                    ssq = sb("ssq", [128, 32], F32, ph)
                    rs8 = sb("rs8", [128, 8], F32, ph)
                    sgvg = rstd
                    sgb = sb("sgb_s", [128, NT], F32, ph)
                    sgw = sb("sgw_s", [128, 8, 128], BF16, ph)
                    junk = sb("junk", [128, 256], BF16, ph)
                    gtmp = [sb("gtmp%d" % i, [128, 512], F32, ph) for i in range(2)]
                    uT = [sb("uT%d" % i, [128, NT], BF16, ph) for i in range(2)]
                    e_c1 = SP.dma(sgvg[:], sgvg_d, ld_c[0])
                    e_c2 = SP.dma(sgb[:], sgb_d, ld_c[1])
                    e_c3 = POOL.dma(sgw[:], sgw_d.rearrange("g q p -> q g p"), ld_c[2], waits=list(bar_evs))

                    def cons_vsg(b, tb, bank, last):
                        e_g = act(VN[:, tb, b * 256:(b + 1) * 256], bank[:, 0:256], AF.Gelu_apprx_tanh, waits=[last])
                        ACT.op(lambda e: e.activation(out=junk[:], in_=VN[:, tb, b * 256:(b + 1) * 256], func=AF.Square,
                                                      accum_out=ssq[:, tb * 4 + b: tb * 4 + b + 1]), waits=[e_g])
                        return e_g
                    proj_tm(wv, 4096, 4, cons_vsg)
                    barrier()
                    e_last = (ACT.s, ACT.s.n)
                    e_s = DVE.op(lambda e: e.tensor_reduce(out=rs8[:], in_=ssq[:].rearrange("p (a b) -> p a b", b=4),
                                                           axis=mybir.AxisListType.X, op=ALU.add), waits=[e_last])
                    e_s = act(rs8[:], rs8[:], AF.Sqrt, waits=[e_s], bias=EPS, scale=1.0 / 1024)
                    e_s = recip(rs8[:], rs8[:], waits=[e_s])
                    for tb in range(8):
                        e_vn = stt(VN[:, tb, :], VN[:, tb, :], rs8[:, tb:tb + 1], sgvg[:], ALU.mult, ALU.mult, waits=[e_s, e_c1])

                    u_free = [None, None]
                    sgring = Ring([(pb[4], pb[5]), (pb[6], pb[7])])
                    sg_pend = []

                    def sg_group(g, e_u):
                        k, (b0, b1), free = sgring.get()
                        last = None
                        for n_ in range(8):
                            bank = b0 if n_ < 4 else b1
                            last = mm(bank[:, (n_ % 4) * 128:(n_ % 4 + 1) * 128], VN[:, n_, g * 128:(g + 1) * 128], sgw[:, g, :],
                                      True, True, waits=[free, e_vn, e_c3], sig=(n_ == 7))
                        ev = None
                        for t, bank in enumerate((b0, b1)):
                            gt = gtmp[t]
                            e_a = tt(gt[:].rearrange("p (n q) -> p n q", q=128), bank[:].rearrange("p (n q) -> p n q", q=128),
                                     sgb[:, g * 128:(g + 1) * 128].unsqueeze(1).broadcast_to([128, 4, 128]), ALU.add,
                                     waits=[last, e_c2])
                            ev = tt(catT[:, 8 + g, t * 512:(t + 1) * 512], gt[:], uT[g % 2][:, t * 512:(t + 1) * 512], ALU.mult,
                                    waits=[e_a, e_u])
                        sgring.rel(k, ev)
                        u_free[g % 2] = ev

                    def cons_u(oc, t, bank, last):
                        w = [last]
                        if t == 0:
                            w.append(u_free[oc % 2])
                        ev = act(uT[oc % 2][:, t * 512:(t + 1) * 512], bank[:], AF.Gelu_apprx_tanh, waits=w)
                        if t == 1:
                            sg_pend.append((oc, ev))
                            if len(sg_pend) > 1:
                                sg_group(*sg_pend.pop(0))
                        return ev
                    proj_fm(wv, 16, 3072, 4, hT, cons_u)
                    while sg_pend:
                        sg_group(*sg_pend.pop(0))
                    barrier()

                with ExitStack() as ph:
                    NB = 8
                    btab = [sb("btab%d" % i, [128, 512], F32, ph) for i in range(NB)]
                    bsem = [sem("bsem%d" % i) for i in range(NB)]
                    stp = [sb("stp%d" % i, [128, 512], F32, ph) for i in range(2)]
                    pT = [sb("opT%d" % i, [128, 512], BF16, ph) for i in range(3)]
                    rden = sb("orden", [128, 512], F32, ph)
                    hflat = hT[:].rearrange("p a b -> p (a b)")
                    KX = [hflat[:, i * 4096:i * 4096 + 1536] for i in range(2)]
                    VX = [hflat[:, i * 4096 + 1536:i * 4096 + 3072].rearrange("p (a b) -> p a b", b=128) for i in range(2)]
                    QX = [hflat[:, i * 4096 + 3072:i * 4096 + 4096] for i in range(2)]
                    kvs = [sem("okvsA"), sem("okvsB")]
                    ev_nb = fetch_nbrs(gathO, NSO, (prevO, nextO), ev_cc)
                    SP.wait(ev_nb)

                    def kview(base, h):
                        return nbr[base][256 * h:256 * h + 256, :].rearrange("(p two) c -> p (two c)", two=2)

                    def kown(h):
                        return slabO[256 * h:256 * h + 256, :].rearrange("(p two) c -> p (two c)", two=2)

                    def vview(base, h):
                        return nbr[base][2048 + 256 * h:2048 + 256 * h + 256, :].rearrange("r (q f) -> (r q) f", f=128)

                    def vown(h):
                        return slabO[2048 + 256 * h:2048 + 256 * h + 256, :].rearrange("r (q f) -> (r q) f", f=128)

                    sring = Ring([pb[0], pb[1], pb[2]])
                    oring = Ring([(pb[4], pb[5]), (pb[6], pb[7])])
                    bt_free = [None] * NB
                    st_free = [None, None]
                    pt_free = [None] * 3
                    kv_read = [None, None]
                    ci = 0
                    for ev in q_evs:
                        POOL.wait(ev)
                    POOL.wait(ev_nb)
                    pend = []
                    bi = [0]
                    ofree_ev = [None, None]
                    lastp_box = [None]
                    c2 = [0]

                    def stage2(it):
                        i2 = c2[0] % 2
                        ip = c2[0] % 3
                        c2[0] += 1
                        ib, sbank, s_, Ob, Db, kt = it["ib"], it["sbank"], it["s_"], it["Ob"], it["Db"], it["kt"]
                        e_a = tt(stp[i2][:], sbank[:], btab[ib][:], ALU.add, waits=[it["l1"], it["e_b"], st_free[i2]])
                        sring.rel(it["ks"], e_a)
                        bt_free[ib] = e_a
                        e_e = act(pT[ip][:], stp[i2][:], AF.Exp, waits=[e_a, pt_free[ip]])
                        st_free[i2] = e_e
                        mm(Ob[:], VX[s_][:, it["et"], :], pT[ip][:], kt == 0, kt == 7, waits=[e_e, ofree_ev[it["pair"]]])
                        lastp = mm(Db[:], ones[:], pT[ip][:], kt == 0, kt == 7, sig=True)
                        pt_free[ip] = lastp
                        lastp_box[0] = lastp
                        if kt == 7:
                            e_r = recip(rden[:], Db[:], waits=[lastp])
                            ofree_ev[it["pair"]] = tt(catT[:, it["h"], 512 * it["qb"]:512 * it["qb"] + 512], Ob[:], rden[:],
                                                      ALU.mult, waits=[e_r])

                    for h in range(8):
                        s_ = h % 2
                        w0 = [kv_read[s_]] + list(bar_evs)
                        evl = []

                        def L(out, in_, w0=w0, s_=s_):
                            evl.append(POOL.dma(out, in_, kvs[s_], waits=w0))
                        L(KX[s_][:, 256:1280], kown(h))
                        L(KX[s_][:, 0:256], oK(nbr[prevO], h)[:, 256:512])
                        L(KX[s_][:, 1280:1536], oK(nbr[nextO], h)[:, 0:256])
                        L(VX[s_][:, 2:10, :], vown(h).rearrange("(j p) f -> p j f", p=128))
                        L(VX[s_][:, 0:2, :], oV(nbr[prevO], h)[256:512, :].rearrange("(j p) f -> p j f", p=128))
                        L(VX[s_][:, 10:12, :], oV(nbr[nextO], h)[0:256, :].rearrange("(j p) f -> p j f", p=128))
                        L(QX[s_][:, :], qscr[h * 128:(h + 1) * 128, :])
                        ev_kv = evl[-1]
                        for qb in range(2):
                            pair = bi[0] % 2
                            bi[0] += 1
                            Ob, Db = oring.items[pair]
                            for kt in range(8):
                                ib = ci % NB
                                ci += 1
                                e_b = SP.dma(btab[ib][:], natab_d[(h * 2 + qb) * 8 + kt], bsem[ib], waits=[bt_free[ib]])
                                ks, sbank, sfree = sring.get()
                                et = 4 * qb + kt
                                l1 = mm(sbank[:], KX[s_][:, 128 * et:128 * et + 128], QX[s_][:, 512 * qb:512 * qb + 512], True, True,
                                        waits=[sfree, ev_kv], sig=True)
                                pend.append(dict(ib=ib, e_b=e_b, ks=ks, sbank=sbank, l1=l1, et=et, kt=kt, s_=s_, Ob=Ob, Db=Db,
                                                 pair=pair, h=h, qb=qb))
                                if len(pend) > 2:
                                    stage2(pend.pop(0))
                        while pend:
                            stage2(pend.pop(0))
                        kv_read[s_] = lastp_box[0]
                    barrier()
                out_proj(owout_d, 16, catT)

        for l in range(2):
            rms_norm_x(l * 4 + 0, False)
            ffn(l * 2 + 0)
            rms_norm_x(l * 4 + 1, False)
            if l == 0:
                even_mixer()
            else:
                odd_mixer()
            rms_norm_x(l * 4 + 2, False)
            ffn(l * 2 + 1)
            rms_norm_x(l * 4 + 3, True)
        ev = None
        for c in range(16):
            ev = SP.dma(out_d[c * 128:(c + 1) * 128, :], xT[:, c, :], stq)
        SP.wait(ev)

        with nc.Block() as block:
            @block.tensor
            def _(e):
                for f in PE.prog:
                    f(e)

            @block.scalar
            def _(e):
                for f in ACT.prog:
                    f(e)

            @block.vector
            def _(e):
                for f in DVE.prog:
                    f(e)

            @block.gpsimd
            def _(e):
                for f in POOL.prog:
                    f(e)

            @block.sync
            def _(e):
                for i, (nm, mx_) in enumerate((("prevE", 7 * NSE), ("nextE", 7 * NSE), ("prevO", 7 * NSO), ("nextO", 7 * NSO))):
                    dyn[nm] = nc.values_load(info_d[0:1, i:i + 1], engines=[mybir.EngineType.SP], min_val=0, max_val=mx_)
                for f in SP.prog:
                    f(e)
    mybir.codegen_inst_isa_subclasses(nc)
    return nc


def _etab():
    t = np.zeros((3, 4, 128, 256), np.float32)
    p = np.arange(128)[:, None]
    i = np.arange(256)[None, :]
    rel = np.abs(64 + p - i)
    for g, d in enumerate((1, 4, 16)):
        for hm in range(4):
            h = 4 * g + hm
            slope = 2.0 ** (-8.0 * (h + 1) / 12)
            t[g, hm] = np.where(rel <= 64, np.exp(-np.float32(slope) * (rel * d).astype(np.float32)), 0.0)
    return np.ascontiguousarray(t.reshape(12, 128, 256).transpose(1, 0, 2).reshape(128, 12 * 256)).astype(ml_dtypes.bfloat16)


def _natab(rpb, s0):
    out = np.empty((8, 2, 8, 128, 512), np.float32)
    for qb in range(2):
        tq = s0 + 512 * qb + np.arange(512)
        qr, qc = tq // 64, tq % 64
        rs = np.clip(qr - 4, 0, 56)
        cs = np.clip(qc - 8, 0, 48)
        for kt in range(8):
            tk = s0 + 128 * (4 * qb + kt) - 256 + np.arange(128)
            inb = (tk >= 0) & (tk < 4096)
            tkc = np.clip(tk, 0, 4095)
            kr, kc = tkc // 64, tkc % 64
            valid = inb[:, None] & (kr[:, None] >= rs[None, :]) & (kr[:, None] < rs[None, :] + 8) \
                & (kc[:, None] >= cs[None, :]) & (kc[:, None] < cs[None, :] + 16)
            rr = np.clip(kr[:, None] - qr[None, :] + 7, 0, 14)
            rc = np.clip(kc[:, None] - qc[None, :], -15, 15) + 15
            for h in range(8):
                out[h, qb, kt] = np.where(valid, rpb[h][rr, rc], np.float32(-1e30))
    return out.reshape(128, 128, 512)


_NC = None


def kernel(x, norm_ffn1, norm_mix, norm_ffn2, norm_out, ffn_w_gate, ffn_w_up, ffn_w_down,
           even_w_in, pool_w, pool_scale, dil_q_gain, dil_k_gain, even_w_out,
           odd_w_in, na_q_gain, na_k_gain, na_rpb, sg_v_gain, sg_w, sg_b, odd_w_out):
    global _NC
    f = lambda a: np.ascontiguousarray(np.asarray(a, dtype=np.float32))
    x = f(x)
    norms = [f(norm_ffn1), f(norm_mix), f(norm_ffn2), f(norm_out)]
    gains = np.zeros((128, 128), np.float32)
    for l in range(2):
        for k in range(4):
            gains[:, (l * 4 + k) * 16:(l * 4 + k + 1) * 16] = norms[k][l].reshape(16, 128).T
    hg = np.stack([f(dil_q_gain)[0], f(dil_k_gain)[0], f(na_q_gain)[0], f(na_k_gain)[0]], axis=1)
    pscale = np.ascontiguousarray(f(pool_scale)[0].reshape(4, 128).T)
    shared = {
        "gains": gains, "hg": np.ascontiguousarray(hg), "pscale": pscale,
        "wg": f(ffn_w_gate).reshape(4, D, DFF), "wu": f(ffn_w_up).reshape(4, D, DFF), "wd": f(ffn_w_down).reshape(4, DFF, D),
        "ewin": f(even_w_in)[0], "ewout": f(even_w_out)[0], "owin": f(odd_w_in)[0], "owout": f(odd_w_out)[0],
        "poolw": f(pool_w)[0], "sgw": np.ascontiguousarray(f(sg_w)[0].transpose(0, 2, 1)),
        "sgb": np.ascontiguousarray(np.broadcast_to(f(sg_b)[0].reshape(1, 1024), (128, 1024))),
        "sgvg": np.ascontiguousarray(np.broadcast_to(f(sg_v_gain)[0].reshape(1, 1024), (128, 1024))),
        "etab": _etab(),
    }
    rpb = f(na_rpb)[0]
    in_maps = []
    for c in range(8):
        b, cp = c // 4, c % 4
        s0 = cp * NT
        m = dict(shared)
        m["xT"] = np.ascontiguousarray(x[b, s0:s0 + NT, :].T)
        vc = np.ones((128, 4), np.float32)
        if cp == 0:
            vc[0:64, 1] = 0.0
        if cp == 3:
            vc[64:128, 2] = 0.0
            vc[:, 3] = 0.0
        m["vcols"] = vc
        pv = np.ones((128, 2), np.float32)
        if cp == 0:
            pv[:, 0] = 0.0
        if cp == 3:
            pv[:, 1] = 0.0
        m["pvalid"] = pv
        pf = np.ones((4, 16), np.float32)
        for g, w in enumerate((2, 4, 8, 16)):
            for j in range(16):
                t = s0 + (j if j < 8 else NT - 16 + j)
                lo = min(max(t - w // 2, 0), 4096)
                hi = min(max(t + w // 2, 0), 4096)
                pf[g, j] = w / float(hi - lo)
        m["poolfix"] = np.ascontiguousarray(np.broadcast_to(pf.reshape(1, 64), (128, 64)))
        m["natab"] = _natab(rpb, s0)
        pr = c - 1 if cp != 0 else c
        nx = c + 1 if cp != 3 else c
        m["info"] = np.array([[pr * NSE, nx * NSE, pr * NSO, nx * NSO]], np.int32)
        in_maps.append(m)
    if _NC is None:
        _NC = build()
    res = run_bass_kernel_spmd(_NC, in_maps, core_ids=list(range(8)))
    out = np.empty((2, 4096, D), np.float32)
    for c in range(8):
        b, cp = c // 4, c % 4
        out[b, cp * NT:(cp + 1) * NT, :] = np.asarray(res.results[c]["outT"]).T
    return out
```

```python
from contextlib import ExitStack
import numpy as np
import ml_dtypes
import concourse.bass as bass
import concourse.mybir as mybir
from concourse.bass import ds
from concourse.bass_utils import run_bass_kernel_spmd

F32 = mybir.dt.float32
BF16 = mybir.dt.bfloat16
I32 = mybir.dt.int32
AF = mybir.ActivationFunctionType
ALU = mybir.AluOpType

NT = 1024
D = 2048
DFF = 5632
NRE = 6160
NRO = 4096
NSE = 3344
NSO = 2048
EPS = 1e-6
NWB = 4


class S:
    def __init__(self, sem):
        self.sem = sem
        self.n = 0


class Q:
    def __init__(self, name):
        self.name = name
        self.prog = []
        self.s = None
        self.waited = {}

    def wait(self, ev):
        if ev is None:
            return
        s, val = ev
        if self.waited.get(id(s), 0) >= val:
            return
        if s is self.s and val <= s.n - 4:
            return
        self.waited[id(s)] = val
        self.prog.append(lambda e, s=s, val=val: e.wait_ge(s.sem, val))

    def op(self, fn, waits=(), sig=True):
        for ev in waits:
            self.wait(ev)
        if sig:
            s = self.s
            s.n += 1
            self.prog.append(lambda e, fn=fn, s=s: fn(e).then_inc(s.sem, 1))
            return (s, s.n)
        self.prog.append(lambda e, fn=fn: fn(e))
        return None

    def dma(self, out, in_, ds_, waits=()):
        for ev in waits:
            self.wait(ev)
        ds_.n += 16
        self.prog.append(lambda e, out=out, in_=in_, ds_=ds_: e.dma_start(
            out=out, in_=(in_() if callable(in_) else in_)).then_inc(ds_.sem, 16))
        return (ds_, ds_.n)


def sl(start, count, step):
    return slice(start, start + step * (count - 1) + 1, step)


class Ring:
    def __init__(self, items):
        self.items = items
        self.free = [None] * len(items)
        self.i = 0

    def get(self):
        k = self.i % len(self.items)
        self.i += 1
        return k, self.items[k], self.free[k]

    def rel(self, k, ev):
        self.free[k] = ev


def build():
    nc = bass.Bass("TRN2", target_bir_lowering=False)

    def din(name, shape, dt=F32):
        return nc.dram_tensor(name, list(shape), dt, kind="ExternalInput").ap()

    xT_d = din("xT", [D, NT])
    gains_d = din("gains", [128, 128])
    hg_d = din("hg", [128, 4])
    pscale_d = din("pscale", [128, 4])
    wg_d = din("wg", [4, D, DFF])
    wu_d = din("wu", [4, D, DFF])
    wd_d = din("wd", [4, DFF, D])
    ewin_d = din("ewin", [D, 5120])
    ewout_d = din("ewout", [1024, D])
    owin_d = din("owin", [D, 5120])
    owout_d = din("owout", [D, D])
    poolw_d = din("poolw", [4, 128, 128])
    sgw_d = din("sgw", [8, 128, 128])
    sgb_d = din("sgb", [128, 1024])
    sgvg_d = din("sgvg", [128, 1024])
    etab_d = din("etab", [128, 12 * 256], BF16)
    vcols_d = din("vcols", [128, 4])
    natab_d = din("natab", [128, 128, 512])
    poolfix_d = din("poolfix", [128, 64])
    pvalid_d = din("pvalid", [128, 2])
    info_d = din("info", [1, 4], I32)
    out_d = nc.dram_tensor("outT", [D, NT], F32, kind="ExternalOutput").ap()
    slabE = nc.dram_tensor("slabE", [NRE, 512], BF16)
    sendE = nc.dram_tensor("sendE", [NSE, 512], BF16)
    gathE = nc.dram_tensor("gathE", [8 * NSE, 512], BF16)
    slabO = nc.dram_tensor("slabO", [NRO, 512], BF16)
    sendO = nc.dram_tensor("sendO", [NSO, 512], BF16)
    gathO = nc.dram_tensor("gathO", [8 * NSO, 512], BF16)
    qscr = nc.dram_tensor("qscr", [12 * 128, NT], BF16)
    nbr = {"prevE": nc.dram_tensor("nb_prevE", [NSE, 512], BF16), "nextE": nc.dram_tensor("nb_nextE", [NSE, 512], BF16),
           "prevO": nc.dram_tensor("nb_prevO", [NSO, 512], BF16), "nextO": nc.dram_tensor("nb_nextO", [NSO, 512], BF16)}

    PE, ACT, DVE, POOL, SP = Q("tensor"), Q("scalar"), Q("vector"), Q("gpsimd"), Q("sync")
    engs = [PE, ACT, DVE, POOL, SP]

    with ExitStack() as st:
        def sem(name):
            return S(st.enter_context(nc.semaphore(name)))

        uid = [0]

        def sb(name, shape, dt, stack=None):
            uid[0] += 1
            return (stack or st).enter_context(nc.sbuf_tensor("s%d_%s" % (uid[0], name), list(shape), dt))

        for q in engs:
            q.s = sem("c_" + q.name)
        cc_sem = sem("cc")

        pb = [st.enter_context(nc.psum_tensor("pb%d" % i, [128, 512], F32)) for i in range(8)]

        xT = sb("xT", [128, 16, NT], F32)
        hT = sb("hT", [128, 16, NT], BF16)
        ones = sb("ones", [128, 128], BF16)
        gains = sb("gains_s", [128, 128], F32)
        hg = sb("hg_s", [128, 6], F32)
        pscale = sb("pscale_s", [128, 4], F32)
        vcols = sb("vcols_s", [128, 4], F32)
        pvalid = sb("pvalid_s", [128, 2], F32)
        poolfix = sb("poolfix_s", [128, 4, 16], F32)
        rstd = sb("rstd", [128, NT], F32)
        sq = [sb("sq%d" % i, [128, NT], BF16) for i in range(2)]
        wslots = [sb("w%d" % i, [128, 4096], BF16) for i in range(NWB)]
        wsem = [sem("wsem%d" % i) for i in range(NWB)]
        wring = Ring(list(range(NWB)))
        ld = sem("ld")
        ld_tab = sem("ld_tab")
        ld_pw = sem("ld_pw")
        ld_h = sem("ld_h")
        ld_c = [sem("ld_c%d" % i) for i in range(3)]
        stq = sem("st")

        bar_evs = []
        dyn = {}
        def barrier():
            evs = [(q.s, q.s.n) for q in engs if q.s.n > 0]
            for q in engs:
                if q is POOL:
                    continue
                for ev in evs:
                    if ev[0] is not q.s:
                        q.wait(ev)
            bar_evs[:] = evs

        def wload(src, shape3):
            k, _, free = wring.get()
            a, b = shape3
            view = wslots[k][:, 0:a * b].rearrange("p (a b) -> p a b", b=b)
            ev = POOL.dma(view, src, wsem[k], waits=[free])
            return view, ev, k

        def mm(out, lhsT, rhs, start, stop, waits=(), sig=False):
            return PE.op(lambda e: e.matmul(out, lhsT=lhsT, rhs=rhs, start=start, stop=stop), waits=waits, sig=sig)

        def act(out, in_, func, waits=(), **kw):
            return ACT.op(lambda e: e.activation(out=out, in_=in_, func=func, **kw), waits=waits)

        def tt(out, in0, in1, op, waits=()):
            return DVE.op(lambda e: e.tensor_tensor(out=out, in0=in0, in1=in1, op=op), waits=waits)

        def stt(out, in0, scalar, in1, op0, op1, waits=()):
            return DVE.op(lambda e: e.scalar_tensor_tensor(out=out, in0=in0, scalar=scalar, in1=in1, op0=op0, op1=op1), waits=waits)

        def ts(out, in0, s1, s2, op0, op1=None, waits=()):
            if op1 is None:
                return DVE.op(lambda e: e.tensor_scalar(out=out, in0=in0, scalar1=s1, scalar2=None, op0=op0), waits=waits)
            return DVE.op(lambda e: e.tensor_scalar(out=out, in0=in0, scalar1=s1, scalar2=s2, op0=op0, op1=op1), waits=waits)

        def recip(out, in_, waits=()):
            return DVE.op(lambda e: e.reciprocal(out=out, in_=in_), waits=waits)

        evs0 = []
        for c in range(16):
            evs0.append(SP.dma(xT[:, c, :], xT_d[c * 128:(c + 1) * 128, :], ld))
        for dst, src in ((gains[:], gains_d), (hg[:, 0:4], hg_d), (pscale[:], pscale_d), (vcols[:], vcols_d),
                         (pvalid[:], pvalid_d), (poolfix[:], poolfix_d.rearrange("p (a b) -> p a b", b=16))):
            evs0.append(SP.dma(dst, src, ld))
        ev_ld0 = evs0[-1]
        DVE.op(lambda e: e.memset(ones[:], 1.0))
        ts(hg[:, 4:5], hg[:, 0:1], 128.0 ** -0.5, None, ALU.mult, waits=[ev_ld0])
        ts(hg[:, 5:6], hg[:, 2:3], 128.0 ** -0.5, None, ALU.mult)
        prevE, nextE, prevO, nextO = "prevE", "nextE", "prevO", "nextO"
        barrier()

        sq_free = [None, None]

        hT_ready = [None, None]

        def rms_norm_x(gidx, final):
            for t in range(2):
                tsl = slice(t * 512, (t + 1) * 512)
                bank = pb[6 + t]
                last = None
                for c in range(16):
                    k = c % 2
                    e_sq = act(sq[k][:, 0:512], xT[:, c, tsl], AF.Square, waits=[sq_free[k]])
                    last = mm(bank[:], ones[:], sq[k][:, 0:512], c == 0, c == 15, waits=[e_sq], sig=True)
                    sq_free[k] = last
                e1 = act(rstd[:, tsl], bank[:], AF.Sqrt, waits=[last], bias=EPS, scale=1.0 / D)
                e2 = recip(rstd[:, tsl], rstd[:, tsl], waits=[e1])
                ev = None
                for c in range(16):
                    dst = xT[:, c, tsl] if final else hT[:, c, tsl]
                    ev = stt(dst, xT[:, c, tsl], gains[:, gidx * 16 + c:gidx * 16 + c + 1], rstd[:, tsl], ALU.mult, ALU.mult, waits=[e2])
                hT_ready[t] = ev
            if final:
                barrier()

        def ffn(li):
            with ExitStack() as ph:
                actT = [sb("actT%d" % i, [128, 2, NT], BF16, ph) for i in range(2)]
                stmp = [sb("stmp%d" % i, [128, 512], F32, ph) for i in range(2)]
                uring = Ring([(pb[0], pb[1]), (pb[2], pb[3])])
                dring = Ring([pb[4], pb[5], pb[6], pb[7]])
                st_free = [None, None]
                NG = DFF // 256
                wgv = wg_d[li].rearrange("(c p) f -> p c f", p=128)
                wuv = wu_d[li].rearrange("(c p) f -> p c f", p=128)
                wdv = wd_d[li].rearrange("(c p) f -> p c f", p=128)
                act_ready = [None] * NG
                act_read = [None, None]
                pend = None

                def down(g, wdblk, wd_ev):
                    last = None
                    for oc in range(16):
                        for t in range(2):
                            k, bank, free = dring.get()
                            for j in range(2):
                                last = mm(bank[:], wdblk[:, j, oc * 128:(oc + 1) * 128],
                                          actT[g % 2][:, j, t * 512:(t + 1) * 512], j == 0, j == 1,
                                          waits=[free, wd_ev, act_ready[g]], sig=(j == 1))
                            ev = stt(xT[:, oc, t * 512:(t + 1) * 512], bank[:], 0.5, xT[:, oc, t * 512:(t + 1) * 512],
                                     ALU.mult, ALU.add, waits=[last])
                            dring.rel(k, ev)
                    return last

                for g in range(NG):
                    gblk, g_ev, gk = wload(wgv[:, :, g * 256:(g + 1) * 256], (16, 256))
                    ublk, u_ev, uk = wload(wuv[:, :, g * 256:(g + 1) * 256], (16, 256))
                    lastu = None
                    for fc in range(2):
                        for t in range(2):
                            k, (bg, bu), free = uring.get()
                            for c in range(16):
                                mm(bg[:], gblk[:, c, fc * 128:(fc + 1) * 128], hT[:, c, t * 512:(t + 1) * 512],
                                   c == 0, c == 15, waits=[free, g_ev, act_read[g % 2], hT_ready[t]])
                            for c in range(16):
                                lastu = mm(bu[:], ublk[:, c, fc * 128:(fc + 1) * 128], hT[:, c, t * 512:(t + 1) * 512],
                                           c == 0, c == 15, waits=[u_ev], sig=(c == 15))
                            sk = (fc * 2 + t) % 2
                            e_s = act(stmp[sk][:], bg[:], AF.Silu, waits=[lastu, st_free[sk]])
                            e_m = tt(actT[g % 2][:, fc, t * 512:(t + 1) * 512], stmp[sk][:], bu[:], ALU.mult, waits=[e_s])
                            st_free[sk] = e_m
                            uring.rel(k, e_m)
                            act_ready[g] = e_m
                    wring.rel(gk, lastu)
                    wring.rel(uk, lastu)
                    if pend is not None:
                        dblk, d_ev, dk = wload(wdv[:, 2 * pend:2 * pend + 2, :], (2, 2048))
                        l = down(pend, dblk, d_ev)
                        wring.rel(dk, l)
                        act_read[pend % 2] = l
                    pend = g
                dblk, d_ev, dk = wload(wdv[:, 2 * pend:2 * pend + 2, :], (2, 2048))
                l = down(pend, dblk, d_ev)
                wring.rel(dk, l)
                barrier()

        def head_units(win_d, col0, nheads, gcol, tmps, emit_out):
            qf, sqh, rt = tmps
            wv = win_d.rearrange("(c p) f -> p c f", p=128)
            mring = Ring([pb[0], pb[1], pb[2], pb[3]])
            sring = Ring([pb[4], pb[5]])
            free_t = [None, None]
            pending = []

            def finish(item):
                h, t, sk, e_sq, e_cp = item
                k2, sbank, free2 = sring.get()
                l2 = mm(sbank[:], ones[:], sqh[sk][:], True, True, waits=[e_sq, free2], sig=True)
                e_r = act(rt[sk][:], sbank[:], AF.Sqrt, waits=[l2], bias=EPS, scale=1.0 / 128)
                sring.rel(k2, e_r)
                e_r2 = recip(rt[sk][:], rt[sk][:], waits=[e_r])
                ev = emit_out(h, t, qf[sk], rt[sk], [e_r2, e_cp])
                free_t[sk] = ev

            ui = 0
            for b in range(nheads // 2):
                blk, b_ev, bk = wload(wv[:, :, col0 + b * 256: col0 + (b + 1) * 256], (16, 256))
                last = None
                for hh in range(2):
                    h = b * 2 + hh
                    for t in range(2):
                        k, bank, free = mring.get()
                        for c in range(16):
                            last = mm(bank[:], blk[:, c, hh * 128:(hh + 1) * 128], hT[:, c, t * 512:(t + 1) * 512],
                                      c == 0, c == 15, waits=[free, b_ev, hT_ready[t]], sig=(c == 15))
                        sk = ui % 2
                        ui += 1
                        e_cp = act(qf[sk][:], bank[:], AF.Copy, waits=[last, free_t[sk]])
                        e_sq = act(sqh[sk][:], bank[:], AF.Square)
                        mring.rel(k, e_sq)
                        pending.append((h, t, sk, e_sq, e_cp))
                        if len(pending) > 1:
                            finish(pending.pop(0))
                wring.rel(bk, last)
            while pending:
                finish(pending.pop(0))

        def proj_fm(w_view, kch, col0, nblk, inT, consume):
            mring = Ring([pb[0], pb[1], pb[2], pb[3]])
            for b in range(nblk):
                blk, b_ev, bk = wload(w_view[:, :, col0 + b * 256: col0 + (b + 1) * 256], (kch, 256))
                last = None
                for hh in range(2):
                    for t in range(2):
                        k, bank, free = mring.get()
                        for c in range(kch):
                            last = mm(bank[:], blk[:, c, hh * 128:(hh + 1) * 128], inT[:, c, t * 512:(t + 1) * 512],
                                      c == 0, c == kch - 1, waits=[free, b_ev, hT_ready[t]], sig=(c == kch - 1))
                        ev = consume(b * 2 + hh, t, bank, last)
                        mring.rel(k, ev)
                wring.rel(bk, last)

        def proj_tm(w_view, col0, nblk, consume):
            mring = Ring([pb[4], pb[5], pb[6], pb[7]])
            for b in range(nblk):
                blk, b_ev, bk = wload(w_view[:, :, col0 + b * 256: col0 + (b + 1) * 256], (16, 256))
                last = None
                for tb in range(8):
                    k, bank, free = mring.get()
                    for c in range(16):
                        last = mm(bank[:, 0:256], hT[:, c, tb * 128:(tb + 1) * 128], blk[:, c, :],
                                  c == 0, c == 15, waits=[free, b_ev, hT_ready[tb // 4]], sig=(c == 15))
                    ev = consume(b, tb, bank, last)
                    mring.rel(k, ev)
                wring.rel(bk, last)

        def out_proj(w_d, kch, catT):
            wv = w_d.rearrange("(c p) f -> p c f", p=128)

            def cons(oc, t, bank, last):
                return tt(xT[:, oc, t * 512:(t + 1) * 512], bank[:], xT[:, oc, t * 512:(t + 1) * 512], ALU.add, waits=[last])
            proj_fm(wv, kch, 0, 8, catT, cons)
            barrier()

        def allgather(slab, gath, evs):
            for ev in evs:
                POOL.wait(ev)
            cc_sem.n += 1
            n = cc_sem.n
            POOL.prog.append(lambda e: e.collective_compute(
                "AllGather", ALU.bypass, replica_groups=[list(range(8))],
                ins=[slab.ap().opt()], outs=[gath.ap().opt()]).then_inc(cc_sem.sem, 1))
            return (cc_sem, n)

        def cK3(t, i): return t[256 * i:256 * i + 256, :].rearrange("(p two) c -> p (two c)", two=2)
        def cV3(t, i): return t[1024 + 256 * i:1024 + 256 * i + 256, :].rearrange("r (q f) -> (r q) f", f=128)
        def cK2(t, i): return t[2048 + 128 * i:2048 + 128 * i + 128, :]
        def cV2(t, i): return t[2560 + 128 * i:2560 + 128 * i + 128, :].rearrange("r (q f) -> (r q) f", f=128)
        def cK1(t, i): return t[3072 + 32 * i:3072 + 32 * i + 32, :].rearrange("r (q f) -> (r q) f", f=128)
        def cV1(t, i): return t[3200 + 32 * i:3200 + 32 * i + 32, :].rearrange("r (q f) -> (r q) f", f=128)
        def cA(t): return t[3328:3344, :].rearrange("r (g k) -> (r g) k", k=16).rearrange("(c p) k -> p c k", p=128)
        def oK(t, h): return t[128 * h:128 * h + 128, :]
        def oV(t, h): return t[1024 + 128 * h:1024 + 128 * h + 128, :].rearrange("r (q f) -> (r q) f", f=128)

        nb_sem = sem("nb")

        def fetch_nbrs(gath, nrows, keys, ev_cc):
            SP.wait(ev_cc)
            ev = None
            for key in keys:
                ev = SP.dma(nbr[key][:, :].rearrange("(a r) c -> a (r c)", a=16),
                            lambda key=key: gath[ds(dyn[key], nrows), :].rearrange("(a r) c -> a (r c)", a=16), nb_sem)
            return ev

        def even_mixer():
            with ExitStack() as mx:
                aT = sb("aT", [128, 4, NT + 16], F32, mx)
                catT = sb("catT", [128, 8, NT], BF16, mx)
                wv = ewin_d.rearrange("(c p) f -> p c f", p=128)
                slab_evs = []
                with ExitStack() as ph:
                    qf = [sb("qf%d" % i, [128, 512], F32, ph) for i in range(2)]
                    sqh = [sb("sqh%d" % i, [128, 512], BF16, ph) for i in range(2)]
                    rt = [sb("rt%d" % i, [128, 512], F32, ph) for i in range(2)]
                    htmp = [sb("htmp%d" % i, [128, NT], BF16, ph) for i in range(2)]
                    vst = [sb("vst%d" % i, [128, 8, 256], BF16, ph) for i in range(2)]
                    aedge = sb("aedge", [128, 4, 16], BF16, ph)
                    ht_free = [None, None]
                    hsem = [sem("hsemA"), sem("hsemB")]

                    a_evs = []

                    def cons_a(oc, t, bank, last):
                        a_evs.append(act(aT[:, oc, 8 + t * 512: 8 + (t + 1) * 512], bank[:], AF.Copy, waits=[last]))
                        return a_evs[-1]
                    proj_fm(wv, 16, 0, 2, hT, cons_a)
                    barrier()
                    ACT.prog.append(lambda e, s=ACT.s, v=a_evs[-1][1]: e.wait_ge(s.sem, v))
                    e1 = act(aedge[:, :, 0:8], aT[:, :, 8:16], AF.Copy)
                    e2 = act(aedge[:, :, 8:16], aT[:, :, NT:NT + 8], AF.Copy)
                    aview = cA(sendE)
                    slab_evs.append(SP.dma(aview, aedge[:], stq, waits=[e1, e2]))

                    q_evs = []

                    def mk_emit(gcol, dst_of_head, evlist, extra=None):
                        def emit(h, t, src, rs, waits):
                            slot = h % 2
                            w = list(waits)
                            if t == 0:
                                w.append(ht_free[slot])
                            ev = stt(htmp[slot][:, t * 512:(t + 1) * 512], src[:], hg[:, gcol:gcol + 1], rs[:],
                                     ALU.mult, ALU.mult, waits=w)
                            if t == 1:
                                dev = SP.dma(dst_of_head(h), htmp[slot][:], hsem[slot], waits=[ev])
                                if extra is not None:
                                    for d_, s_ in extra(h, htmp[slot]):
                                        dev = SP.dma(d_, s_, hsem[slot], waits=[ev])
                                ht_free[slot] = dev
                                evlist.append(dev)
                            return ev
                        return emit

                    def k_extra(h, t_):
                        if h < 4:
                            return [(cK1(sendE, h)[:, 0:64], t_[:, 0:64]), (cK1(sendE, h)[:, 64:128], t_[:, 960:1024])]
                        if h < 8:
                            return [(cK2(sendE, h - 4)[:, 0:256], t_[:, 0:256]), (cK2(sendE, h - 4)[:, 256:512], t_[:, 768:1024])]
                        return [(cK3(sendE, h - 8), t_[:])]
                    head_units(ewin_d, 2048, 12, 1, (qf, sqh, rt), mk_emit(
                        1, lambda h: slabE[256 * h:256 * h + 256, :].rearrange("(p two) c -> p (two c)", two=2), slab_evs, k_extra))
                    barrier()

                    vs_free = [None, None]
                    vsem = [sem("vsemA"), sem("vsemB")]

                    def cons_v(b, tb, bank, last):
                        w = [last]
                        if tb == 0:
                            w.append(vs_free[b % 2])
                        ev = act(vst[b % 2][:, tb, :], bank[:, 0:256], AF.Copy, waits=w)
                        if tb == 7:
                            for j in range(2):
                                h = 2 * b + j
                                dst = slabE[3072 + 256 * h:3072 + 256 * h + 256, :].rearrange(
                                    "r (q f) -> (r q) f", f=128).rearrange("(tb p) f -> p tb f", p=128)
                                vs_ = vst[b % 2]
                                jc = slice(j * 128, (j + 1) * 128)
                                dev = SP.dma(dst, vs_[:, :, jc], vsem[b % 2], waits=[ev])
                                if h < 4:
                                    SP.dma(cV1(sendE, h)[0:64, :], vs_[0:64, 0, jc], vsem[b % 2])
                                    dev = SP.dma(cV1(sendE, h)[64:128, :], vs_[64:128, 7, jc], vsem[b % 2])
                                elif h < 8:
                                    c2 = cV2(sendE, h - 4).rearrange("(tb p) f -> p tb f", p=128)
                                    SP.dma(c2[:, 0:2, :], vs_[:, 0:2, jc], vsem[b % 2])
                                    dev = SP.dma(c2[:, 2:4, :], vs_[:, 6:8, jc], vsem[b % 2])
                                else:
                                    dev = SP.dma(cV3(sendE, h - 8).rearrange("(tb p) f -> p tb f", p=128), vs_[:, :, jc], vsem[b % 2])
                                slab_evs.append(dev)
                            vs_free[b % 2] = dev
                        return ev
                    proj_tm(wv, 3584, 6, cons_v)
                    barrier()
                    ev_cc = allgather(sendE, gathE, slab_evs)
                    head_units(ewin_d, 512, 12, 4, (qf, sqh, rt), mk_emit(4, lambda h: qscr[h * 128:(h + 1) * 128, :], q_evs))
                    barrier()
                for ev in q_evs:
                    SP.wait(ev)

                with ExitStack() as ph:
                    etab = sb("etab", [128, 12, 256], BF16, ph)
                    pooled = [sb("pooled%d" % i, [128, NT], BF16, ph) for i in range(2)]
                    s_a = sb("s_a", [128, NT + 16], F32, ph)
                    s_b = sb("s_b", [128, NT + 16], F32, ph)
                    ahalo = sb("ahalo", [128, 4, 16], BF16, ph)
                    pexp = [sb("pexp%d" % i, [128, 512], F32, ph) for i in range(2)]
                    pT = [sb("pT%d" % i, [128, 512], BF16, ph) for i in range(3)]
                    rden = sb("rden", [128, 512], F32, ph)
                    pw = sb("pw", [128, 4, 128], BF16, ph)
                    hflat = hT[:].rearrange("p a b -> p (a b)")
                    o = 0

                    def carve(n):
                        nonlocal o
                        v = hflat[:, o:o + n]
                        o += n
                        return v
                    KT1 = carve(1152)
                    KT2 = carve(1536)
                    KT3 = carve(3072)
                    V1 = carve(9 * 128).rearrange("p (a b) -> p a b", b=128)
                    V2 = carve(12 * 128).rearrange("p (a b) -> p a b", b=128)
                    V3 = carve(32 * 128).rearrange("p (a b) -> p a b", b=128)
                    QT = carve(3 * NT).rearrange("p (a b) -> p a b", b=NT)
                    kvs = sem("kvs")

                    e_tab = SP.dma(etab[:], etab_d.rearrange("p (a b) -> p a b", b=256), ld_tab)
                    e_pw = POOL.dma(pw[:], poolw_d.rearrange("g c e -> c g e"), ld_pw, waits=list(bar_evs))
                    ev_nb = fetch_nbrs(gathE, NSE, (prevE, nextE), ev_cc)
                    SP.wait(ev_nb)

                    def av(key):
                        return cA(nbr[key])
                    eh1 = SP.dma(ahalo[:, :, 0:8], lambda: av(prevE)[:, :, 8:16], ld_h)
                    eh2 = SP.dma(ahalo[:, :, 8:16], lambda: av(nextE)[:, :, 0:8], ld_h)
                    ts(aT[:, :, 0:8], ahalo[:, :, 0:8], pvalid[:, 0:1], None, ALU.mult, waits=[eh1, eh2])
                    e_h = ts(aT[:, :, NT + 8:NT + 16], ahalo[:, :, 8:16], pvalid[:, 1:2], None, ALU.mult)
                    pring = Ring([pb[0], pb[1]])
                    po_free = [None, None]
                    for g, w in enumerate((2, 4, 8, 16)):
                        L = NT + 16
                        src = aT[:, g, :]
                        cur, nxt = s_a, s_b
                        step = 1
                        n = L
                        first = True
                        e_p = e_h
                        while step < w:
                            n = n - step
                            a_in = src if first else cur
                            e_p = tt(nxt[:, 0:n], a_in[:, 0:n], a_in[:, step:step + n], ALU.add, waits=[e_p])
                            cur, nxt = nxt, cur
                            first = False
                            step *= 2
                        o0 = 8 - w // 2
                        e_f1 = tt(cur[:, o0:o0 + 8], cur[:, o0:o0 + 8], poolfix[:, g, 0:8], ALU.mult, waits=[e_p])
                        e_f2 = tt(cur[:, o0 + NT - 8:o0 + NT], cur[:, o0 + NT - 8:o0 + NT], poolfix[:, g, 8:16], ALU.mult, waits=[e_f1])
                        e_po = stt(pooled[g % 2][:], cur[:, o0:o0 + NT], 1.0 / w, aT[:, g, 8:8 + NT], ALU.mult, ALU.subtract,
                                   waits=[e_f2, po_free[g % 2]])
                        last = None
                        for t in range(2):
                            k, bank, free = pring.get()
                            last = mm(bank[:], pw[:, g, :], pooled[g % 2][:, t * 512:(t + 1) * 512], True, True,
                                      waits=[free, e_po, e_pw], sig=True)
                            ev = act(catT[:, g, t * 512:(t + 1) * 512], bank[:], AF.Identity, waits=[last], scale=pscale[:, g:g + 1])
                            pring.rel(k, ev)
                        po_free[g % 2] = last
                    barrier()

                    def kview(base, h):
                        return nbr[base][256 * h:256 * h + 256, :].rearrange("(p two) c -> p (two c)", two=2)

                    def kown(h):
                        return slabE[256 * h:256 * h + 256, :].rearrange("(p two) c -> p (two c)", two=2)

                    def vview(base, h):
                        return nbr[base][3072 + 256 * h:3072 + 256 * h + 256, :].rearrange("r (q f) -> (r q) f", f=128)

                    def vown(h):
                        return slabE[3072 + 256 * h:3072 + 256 * h + 256, :].rearrange("r (q f) -> (r q) f", f=128)

                    sring = Ring([pb[0], pb[1], pb[2]])
                    oring = Ring([(pb[4], pb[5]), (pb[6], pb[7])])
                    pe_free = [None, None]
                    pt_free = [None, None, None]
                    cnt = [0, 0]
                    kv_read = None
                    pend = []
                    bi = [0]
                    ofree_ev = [None, None]
                    lastp_box = [None]

                    def stage2(it):
                        kp, n, nres, sbank = it["kp"], it["n"], it["nres"], it["sbank"]
                        tot = nres * n
                        ie = cnt[0] % 2
                        cnt[0] += 1
                        e_e = act(pexp[ie][0:kp, 0:tot], sbank[0:kp, 0:tot], AF.Exp, waits=[it["last"], pe_free[ie]])
                        sring.rel(it["ks"], e_e)
                        ip = cnt[1] % 3
                        cnt[1] += 1
                        if nres == 1:
                            tabv = it["tab"]
                            pin = pexp[ie][0:kp, 0:tot]
                            pout = pT[ip][0:kp, 0:tot]
                        else:
                            tabv = it["tab"].unsqueeze(1).broadcast_to([kp, nres, n])
                            pin = pexp[ie][0:kp, 0:tot].rearrange("p (r n) -> p r n", n=n)
                            pout = pT[ip][0:kp, 0:tot].rearrange("p (r n) -> p r n", n=n)
                        e_p = stt(pout, pin, vcols[0:kp, it["vcol"]:it["vcol"] + 1], tabv, ALU.mult, ALU.mult,
                                  waits=[e_e, pt_free[ip]])
                        pe_free[ie] = e_p
                        blk = it["blk"]
                        lastp = None
                        for r in range(nres):
                            rhs = pT[ip][0:kp, r * n:(r + 1) * n]
                            mm(it["oview"][r], it["vl"][r], rhs, blk["first"], False, waits=[e_p, ofree_ev[blk["pair"]]])
                            lastp = mm(it["dview"][r], ones[0:kp, :], rhs, blk["first"], False, sig=(r == nres - 1))
                            blk["first"] = False
                        pt_free[ip] = lastp
                        lastp_box[0] = lastp
                        if it["post"] is not None:
                            it["post"](lastp)
                    for hm in range(4):
                        ld_evs = []
                        w0 = [kv_read]
                        h1, h2, h3 = hm, 4 + hm, 8 + hm

                        def L(out, in_, w0=w0):
                            ld_evs.append(SP.dma(out, in_, kvs, waits=w0))
                        L(KT1[:, 64:1088], kown(h1))
                        L(KT1[:, 0:64], cK1(nbr[prevE], hm)[:, 64:128])
                        L(KT1[:, 1088:1152], cK1(nbr[nextE], hm)[:, 0:64])
                        L(KT2[:, 256:1280], kown(h2))
                        L(KT2[:, 0:256], cK2(nbr[prevE], hm)[:, 256:512])
                        L(KT2[:, 1280:1536], cK2(nbr[nextE], hm)[:, 0:256])
                        L(KT3[:, 1024:2048], kown(h3))
                        L(KT3[:, 0:1024], cK3(nbr[prevE], hm))
                        L(KT3[:, 2048:3072], cK3(nbr[nextE], hm))
                        for i, h in enumerate((h1, h2, h3)):
                            L(QT[:, i, :], qscr[h * 128:(h + 1) * 128, :])
                        vo = vown(h1)
                        L(V1[64:128, 0, :], vo[0:64, :])
                        L(V1[:, 1:8, :], vo[64:960, :].rearrange("(j p) f -> p j f", p=128))
                        L(V1[0:64, 8, :], vo[960:1024, :])
                        L(V1[0:64, 0, :], cV1(nbr[prevE], hm)[64:128, :])
                        L(V1[64:128, 8, :], cV1(nbr[nextE], hm)[0:64, :])
                        vo = vown(h2)
                        V2v = V2.rearrange("p (r j) f -> p r j f", j=3)
                        vo4 = vo.rearrange("(u r) f -> u r f", r=4)
                        L(V2v[64:128, :, 0, :], vo4[0:64])
                        L(V2v[:, :, 1, :], vo4[64:192])
                        L(V2v[0:64, :, 2, :], vo4[192:256])
                        L(V2v[0:64, :, 0, :], cV2(nbr[prevE], hm)[256:512, :].rearrange("(u r) f -> u r f", r=4))
                        L(V2v[64:128, :, 2, :], cV2(nbr[nextE], hm)[0:256, :].rearrange("(u r) f -> u r f", r=4))
                        vo = vown(h3)
                        V3v = V3.rearrange("p (r j) f -> p r j f", j=2)
                        L(V3v[0:64, :, 0, :], cV3(nbr[prevE], hm).rearrange("(u r) f -> u r f", r=16))
                        L(V3v[64:128, :, 0, :], vo.rearrange("(u r) f -> u r f", r=16))
                        L(V3v[0:64, :, 1, :], cV3(nbr[nextE], hm).rearrange("(u r) f -> u r f", r=16))
                        ev_kv = ld_evs[-1]

                        for qb in range(2):
                            Q0 = 512 * qb
                            pair = bi[0] % 2
                            bi[0] += 1
                            Ob, Db = oring.items[pair]
                            blk = {"first": True, "pair": pair}

                            def tile(klhs, qrhs, kp, n, tab, vcol, vl, oview, dview, nres=1, blk=blk):
                                ks, sbank, sfree = sring.get()
                                last = None
                                for r in range(nres):
                                    last = mm(sbank[0:kp, r * n:(r + 1) * n], klhs[r], qrhs[r], True, True,
                                              waits=[sfree, ev_kv, e_tab], sig=(r == nres - 1))
                                item = dict(ks=ks, sbank=sbank, last=last, kp=kp, n=n, nres=nres, tab=tab, vcol=vcol,
                                            vl=vl, oview=oview, dview=dview, blk=blk, post=None)
                                pend.append(item)
                                if len(pend) > 2:
                                    stage2(pend.pop(0))
                                return item

                            lastp = None
                            for j in range(4 * qb, 4 * qb + 5):
                                qlo = max(128 * (j - 1), Q0)
                                qhi = min(128 * (j + 1), Q0 + 512)
                                n = qhi - qlo
                                i0 = qlo - 128 * (j - 1)
                                vcol = 1 if j == 0 else (2 if j == 8 else 0)
                                it_ = tile([KT1[:, 128 * j:128 * j + 128]], [QT[:, 0, qlo:qhi]], 128, n,
                                             etab[:, hm, i0:i0 + n], vcol, [V1[:, j, :]],
                                             [Ob[:, qlo - Q0:qhi - Q0]], [Db[:, qlo - Q0:qhi - Q0]])
                            for (j, i0) in (((0, 128), (1, 0)) if qb == 0 else ((1, 128), (2, 0))):
                                u0 = 128 * qb
                                vcol = 1 if j == 0 else (2 if j == 2 else 0)
                                kl = [KT2[:, sl(512 * j + r, 128, 4)] for r in range(4)]
                                qr = [QT[:, 1, sl(4 * u0 + r, 128, 4)] for r in range(4)]
                                vl = [V2[:, r * 3 + j, :] for r in range(4)]
                                ov = [Ob[:, sl(r, 128, 4)] for r in range(4)]
                                dv = [Db[:, sl(r, 128, 4)] for r in range(4)]
                                it_ = tile(kl, qr, 128, 128, etab[:, 4 + hm, i0:i0 + 128], vcol, vl, ov, dv, nres=4)
                            uq0 = 32 * qb
                            qr = [QT[:, 2, sl(Q0 + r, 32, 16)] for r in range(16)]
                            ov = [Ob[:, sl(r, 32, 16)] for r in range(16)]
                            dv = [Db[:, sl(r, 32, 16)] for r in range(16)]
                            kl = [KT3[:, sl(r, 128, 16)] for r in range(16)]
                            vl = [V3[:, r * 2 + 0, :] for r in range(16)]
                            it_ = tile(kl, qr, 128, 32, etab[:, 8 + hm, 128 + uq0:128 + uq0 + 32], 1, vl, ov, dv, nres=16)
                            kl = [KT3[:, sl(2048 + r, 64, 16)] for r in range(16)]
                            vl = [V3[0:64, r * 2 + 1, :] for r in range(16)]
                            it_ = tile(kl, qr, 64, 32, etab[0:64, 8 + hm, uq0:uq0 + 32], 3, vl, ov, dv, nres=16)

                            def fin(lastp, Ob=Ob, Db=Db, hm=hm, Q0=Q0, pair=pair):
                                e_r = recip(rden[:], Db[:], waits=[lastp])
                                ofree_ev[pair] = tt(catT[:, 4 + hm, Q0:Q0 + 512], Ob[:], rden[:], ALU.mult, waits=[e_r])
                            it_["post"] = fin
                        while pend:
                            stage2(pend.pop(0))
                        kv_read = lastp_box[0]
                    barrier()
                out_proj(ewout_d, 8, catT)

        def odd_mixer():
            with ExitStack() as mx:
                catT = sb("catTo", [128, 16, NT], BF16, mx)
                wv = owin_d.rearrange("(c p) f -> p c f", p=128)
                slab_evs = []
                with ExitStack() as ph:
                    qf = [sb("oqf%d" % i, [128, 512], F32, ph) for i in range(2)]
                    sqh = [sb("osqh%d" % i, [128, 512], BF16, ph) for i in range(2)]
                    rt = [sb("ort%d" % i, [128, 512], F32, ph) for i in range(2)]
                    htmp = [sb("ohtmp%d" % i, [128, NT], BF16, ph) for i in range(2)]
                    vst = [sb("ovst%d" % i, [128, 8, 256], BF16, ph) for i in range(2)]
                    ht_free = [None, None]
                    hsem = [sem("ohsemA"), sem("ohsemB")]

                    q_evs = []

                    def mk_emit(gcol, dst_of_head, evlist, extra=None):
                        def emit(h, t, src, rs, waits):
                            slot = h % 2
                            w = list(waits)
                            if t == 0:
                                w.append(ht_free[slot])
                            ev = stt(htmp[slot][:, t * 512:(t + 1) * 512], src[:], hg[:, gcol:gcol + 1], rs[:],
                                     ALU.mult, ALU.mult, waits=w)
                            if t == 1:
                                dev = SP.dma(dst_of_head(h), htmp[slot][:], hsem[slot], waits=[ev])
                                if extra is not None:
                                    for d_, s_ in extra(h, htmp[slot]):
                                        dev = SP.dma(d_, s_, hsem[slot], waits=[ev])
                                ht_free[slot] = dev
                                evlist.append(dev)
                            return ev
                        return emit
                    head_units(owin_d, 1024, 8, 3, (qf, sqh, rt), mk_emit(
                        3, lambda h: slabO[256 * h:256 * h + 256, :].rearrange("(p two) c -> p (two c)", two=2), slab_evs,
                        lambda h, t_: [(oK(sendO, h)[:, 0:256], t_[:, 0:256]), (oK(sendO, h)[:, 256:512], t_[:, 768:1024])]))
                    barrier()

                    vs_free = [None, None]
                    vsem = [sem("ovsemA"), sem("ovsemB")]

                    def cons_v(b, tb, bank, last):
                        w = [last]
                        if tb == 0:
                            w.append(vs_free[b % 2])
                        ev = act(vst[b % 2][:, tb, :], bank[:, 0:256], AF.Copy, waits=w)
                        if tb == 7:
                            for j in range(2):
                                h = 2 * b + j
                                dst = slabO[2048 + 256 * h:2048 + 256 * h + 256, :].rearrange(
                                    "r (q f) -> (r q) f", f=128).rearrange("(tb p) f -> p tb f", p=128)
                                vs_ = vst[b % 2]
                                jc = slice(j * 128, (j + 1) * 128)
                                SP.dma(dst, vs_[:, :, jc], vsem[b % 2], waits=[ev])
                                c2 = oV(sendO, h).rearrange("(tb p) f -> p tb f", p=128)
                                SP.dma(c2[:, 0:2, :], vs_[:, 0:2, jc], vsem[b % 2])
                                dev = SP.dma(c2[:, 2:4, :], vs_[:, 6:8, jc], vsem[b % 2])
                                slab_evs.append(dev)
                            vs_free[b % 2] = dev
                        return ev
                    proj_tm(wv, 2048, 4, cons_v)
                    barrier()
                    ev_cc = allgather(sendO, gathO, slab_evs)
                    head_units(owin_d, 0, 8, 5, (qf, sqh, rt), mk_emit(5, lambda h: qscr[h * 128:(h + 1) * 128, :], q_evs))
                    barrier()

                with ExitStack() as ph:
                    VN = sb("VN", [128, 8, NT], BF16, ph)
                    ssq = sb("ssq", [128, 32], F32, ph)
                    rs8 = sb("rs8", [128, 8], F32, ph)
                    sgvg = rstd
                    sgb = sb("sgb_s", [128, NT], F32, ph)
                    sgw = sb("sgw_s", [128, 8, 128], BF16, ph)
                    junk = sb("junk", [128, 256], BF16, ph)
                    gtmp = [sb("gtmp%d" % i, [128, 512], F32, ph) for i in range(2)]
                    uT = [sb("uT%d" % i, [128, NT], BF16, ph) for i in range(2)]
                    e_c1 = SP.dma(sgvg[:], sgvg_d, ld_c[0])
                    e_c2 = SP.dma(sgb[:], sgb_d, ld_c[1])
                    e_c3 = POOL.dma(sgw[:], sgw_d.rearrange("g q p -> q g p"), ld_c[2], waits=list(bar_evs))

                    def cons_vsg(b, tb, bank, last):
                        e_g = act(VN[:, tb, b * 256:(b + 1) * 256], bank[:, 0:256], AF.Gelu_apprx_tanh, waits=[last])
                        ACT.op(lambda e: e.activation(out=junk[:], in_=VN[:, tb, b * 256:(b + 1) * 256], func=AF.Square,
                                                      accum_out=ssq[:, tb * 4 + b: tb * 4 + b + 1]), waits=[e_g])
                        return e_g
                    proj_tm(wv, 4096, 4, cons_vsg)
                    barrier()
                    e_last = (ACT.s, ACT.s.n)
                    e_s = DVE.op(lambda e: e.tensor_reduce(out=rs8[:], in_=ssq[:].rearrange("p (a b) -> p a b", b=4),
                                                           axis=mybir.AxisListType.X, op=ALU.add), waits=[e_last])
                    e_s = act(rs8[:], rs8[:], AF.Sqrt, waits=[e_s], bias=EPS, scale=1.0 / 1024)
                    e_s = recip(rs8[:], rs8[:], waits=[e_s])
                    for tb in range(8):
                        e_vn = stt(VN[:, tb, :], VN[:, tb, :], rs8[:, tb:tb + 1], sgvg[:], ALU.mult, ALU.mult, waits=[e_s, e_c1])

                    u_free = [None, None]
                    sgring = Ring([(pb[4], pb[5]), (pb[6], pb[7])])
                    sg_pend = []

                    def sg_group(g, e_u):
                        k, (b0, b1), free = sgring.get()
                        last = None
                        for n_ in range(8):
                            bank = b0 if n_ < 4 else b1
                            last = mm(bank[:, (n_ % 4) * 128:(n_ % 4 + 1) * 128], VN[:, n_, g * 128:(g + 1) * 128], sgw[:, g, :],
                                      True, True, waits=[free, e_vn, e_c3], sig=(n_ == 7))
                        ev = None
                        for t, bank in enumerate((b0, b1)):
                            gt = gtmp[t]
                            e_a = tt(gt[:].rearrange("p (n q) -> p n q", q=128), bank[:].rearrange("p (n q) -> p n q", q=128),
                                     sgb[:, g * 128:(g + 1) * 128].unsqueeze(1).broadcast_to([128, 4, 128]), ALU.add,
                                     waits=[last, e_c2])
                            ev = tt(catT[:, 8 + g, t * 512:(t + 1) * 512], gt[:], uT[g % 2][:, t * 512:(t + 1) * 512], ALU.mult,
                                    waits=[e_a, e_u])
                        sgring.rel(k, ev)
                        u_free[g % 2] = ev

                    def cons_u(oc, t, bank, last):
                        w = [last]
                        if t == 0:
                            w.append(u_free[oc % 2])
                        ev = act(uT[oc % 2][:, t * 512:(t + 1) * 512], bank[:], AF.Gelu_apprx_tanh, waits=w)
                        if t == 1:
                            sg_pend.append((oc, ev))
                            if len(sg_pend) > 1:
                                sg_group(*sg_pend.pop(0))
                        return ev
                    proj_fm(wv, 16, 3072, 4, hT, cons_u)
                    while sg_pend:
                        sg_group(*sg_pend.pop(0))
                    barrier()

                with ExitStack() as ph:
                    NB = 8
                    btab = [sb("btab%d" % i, [128, 512], F32, ph) for i in range(NB)]
                    bsem = [sem("bsem%d" % i) for i in range(NB)]
                    stp = [sb("stp%d" % i, [128, 512], F32, ph) for i in range(2)]
                    pT = [sb("opT%d" % i, [128, 512], BF16, ph) for i in range(3)]
                    rden = sb("orden", [128, 512], F32, ph)
                    hflat = hT[:].rearrange("p a b -> p (a b)")
                    KX = [hflat[:, i * 4096:i * 4096 + 1536] for i in range(2)]
                    VX = [hflat[:, i * 4096 + 1536:i * 4096 + 3072].rearrange("p (a b) -> p a b", b=128) for i in range(2)]
                    QX = [hflat[:, i * 4096 + 3072:i * 4096 + 4096] for i in range(2)]
                    kvs = [sem("okvsA"), sem("okvsB")]
                    ev_nb = fetch_nbrs(gathO, NSO, (prevO, nextO), ev_cc)
                    SP.wait(ev_nb)

                    def kview(base, h):
                        return nbr[base][256 * h:256 * h + 256, :].rearrange("(p two) c -> p (two c)", two=2)

                    def kown(h):
                        return slabO[256 * h:256 * h + 256, :].rearrange("(p two) c -> p (two c)", two=2)

                    def vview(base, h):
                        return nbr[base][2048 + 256 * h:2048 + 256 * h + 256, :].rearrange("r (q f) -> (r q) f", f=128)

                    def vown(h):
                        return slabO[2048 + 256 * h:2048 + 256 * h + 256, :].rearrange("r (q f) -> (r q) f", f=128)

                    sring = Ring([pb[0], pb[1], pb[2]])
                    oring = Ring([(pb[4], pb[5]), (pb[6], pb[7])])
                    bt_free = [None] * NB
                    st_free = [None, None]
                    pt_free = [None] * 3
                    kv_read = [None, None]
                    ci = 0
                    for ev in q_evs:
                        SP.wait(ev)
                    SP.wait(ev_nb)
                    pend = []
                    bi = [0]
                    ofree_ev = [None, None]
                    lastp_box = [None]
                    c2 = [0]

                    def stage2(it):
                        i2 = c2[0] % 2
                        ip = c2[0] % 3
                        c2[0] += 1
                        ib, sbank, s_, Ob, Db, kt = it["ib"], it["sbank"], it["s_"], it["Ob"], it["Db"], it["kt"]
                        e_a = tt(stp[i2][:], sbank[:], btab[ib][:], ALU.add, waits=[it["l1"], it["e_b"], st_free[i2]])
                        sring.rel(it["ks"], e_a)
                        bt_free[ib] = e_a
                        e_e = act(pT[ip][:], stp[i2][:], AF.Exp, waits=[e_a, pt_free[ip]])
                        st_free[i2] = e_e
                        mm(Ob[:], VX[s_][:, it["et"], :], pT[ip][:], kt == 0, kt == 7, waits=[e_e, ofree_ev[it["pair"]]])
                        lastp = mm(Db[:], ones[:], pT[ip][:], kt == 0, kt == 7, sig=True)
                        pt_free[ip] = lastp
                        lastp_box[0] = lastp
                        if kt == 7:
                            e_r = recip(rden[:], Db[:], waits=[lastp])
                            ofree_ev[it["pair"]] = tt(catT[:, it["h"], 512 * it["qb"]:512 * it["qb"] + 512], Ob[:], rden[:],
                                                      ALU.mult, waits=[e_r])

                    for h in range(8):
                        s_ = h % 2
                        w0 = [kv_read[s_]] + list(bar_evs)
                        evl = []

                        def L(out, in_, w0=w0, s_=s_):
                            evl.append(SP.dma(out, in_, kvs[s_], waits=w0))
                        L(KX[s_][:, 256:1280], kown(h))
                        L(KX[s_][:, 0:256], oK(nbr[prevO], h)[:, 256:512])
                        L(KX[s_][:, 1280:1536], oK(nbr[nextO], h)[:, 0:256])
                        L(VX[s_][:, 2:10, :], vown(h).rearrange("(j p) f -> p j f", p=128))
                        L(VX[s_][:, 0:2, :], oV(nbr[prevO], h)[256:512, :].rearrange("(j p) f -> p j f", p=128))
                        L(VX[s_][:, 10:12, :], oV(nbr[nextO], h)[0:256, :].rearrange("(j p) f -> p j f", p=128))
                        L(QX[s_][:, :], qscr[h * 128:(h + 1) * 128, :])
                        ev_kv = evl[-1]
                        for qb in range(2):
                            pair = bi[0] % 2
                            bi[0] += 1
                            Ob, Db = oring.items[pair]
                            for kt in range(8):
                                ib = ci % NB
                                ci += 1
                                e_b = SP.dma(btab[ib][:], natab_d[(h * 2 + qb) * 8 + kt], bsem[ib], waits=[bt_free[ib]])
                                ks, sbank, sfree = sring.get()
                                et = 4 * qb + kt
                                l1 = mm(sbank[:], KX[s_][:, 128 * et:128 * et + 128], QX[s_][:, 512 * qb:512 * qb + 512], True, True,
                                        waits=[sfree, ev_kv], sig=True)
                                pend.append(dict(ib=ib, e_b=e_b, ks=ks, sbank=sbank, l1=l1, et=et, kt=kt, s_=s_, Ob=Ob, Db=Db,
                                                 pair=pair, h=h, qb=qb))
                                if len(pend) > 2:
                                    stage2(pend.pop(0))
                        while pend:
                            stage2(pend.pop(0))
                        kv_read[s_] = lastp_box[0]
                    barrier()
                out_proj(owout_d, 16, catT)

        for l in range(2):
            rms_norm_x(l * 4 + 0, False)
            ffn(l * 2 + 0)
            rms_norm_x(l * 4 + 1, False)
            if l == 0:
                even_mixer()
            else:
                odd_mixer()
            rms_norm_x(l * 4 + 2, False)
            ffn(l * 2 + 1)
            rms_norm_x(l * 4 + 3, True)
        ev = None
        for c in range(16):
            ev = SP.dma(out_d[c * 128:(c + 1) * 128, :], xT[:, c, :], stq)
        SP.wait(ev)

        with nc.Block() as block:
            @block.tensor
            def _(e):
                for f in PE.prog:
                    f(e)

            @block.scalar
            def _(e):
                for f in ACT.prog:
                    f(e)

            @block.vector
            def _(e):
                for f in DVE.prog:
                    f(e)

            @block.gpsimd
            def _(e):
                for f in POOL.prog:
                    f(e)

            @block.sync
            def _(e):
                for i, (nm, mx_) in enumerate((("prevE", 7 * NSE), ("nextE", 7 * NSE), ("prevO", 7 * NSO), ("nextO", 7 * NSO))):
                    dyn[nm] = nc.values_load(info_d[0:1, i:i + 1], engines=[mybir.EngineType.SP], min_val=0, max_val=mx_)
                for f in SP.prog:
                    f(e)
    mybir.codegen_inst_isa_subclasses(nc)
    return nc


def _etab():
    t = np.zeros((3, 4, 128, 256), np.float32)
    p = np.arange(128)[:, None]
    i = np.arange(256)[None, :]
    rel = np.abs(64 + p - i)
    for g, d in enumerate((1, 4, 16)):
        for hm in range(4):
            h = 4 * g + hm
            slope = 2.0 ** (-8.0 * (h + 1) / 12)
            t[g, hm] = np.where(rel <= 64, np.exp(-np.float32(slope) * (rel * d).astype(np.float32)), 0.0)
    return np.ascontiguousarray(t.reshape(12, 128, 256).transpose(1, 0, 2).reshape(128, 12 * 256)).astype(ml_dtypes.bfloat16)


def _natab(rpb, s0):
    out = np.empty((8, 2, 8, 128, 512), np.float32)
    for qb in range(2):
        tq = s0 + 512 * qb + np.arange(512)
        qr, qc = tq // 64, tq % 64
        rs = np.clip(qr - 4, 0, 56)
        cs = np.clip(qc - 8, 0, 48)
        for kt in range(8):
            tk = s0 + 128 * (4 * qb + kt) - 256 + np.arange(128)
            inb = (tk >= 0) & (tk < 4096)
            tkc = np.clip(tk, 0, 4095)
            kr, kc = tkc // 64, tkc % 64
            valid = inb[:, None] & (kr[:, None] >= rs[None, :]) & (kr[:, None] < rs[None, :] + 8) \
                & (kc[:, None] >= cs[None, :]) & (kc[:, None] < cs[None, :] + 16)
            rr = np.clip(kr[:, None] - qr[None, :] + 7, 0, 14)
            rc = np.clip(kc[:, None] - qc[None, :], -15, 15) + 15
            for h in range(8):
                out[h, qb, kt] = np.where(valid, rpb[h][rr, rc], np.float32(-1e30))
    return out.reshape(128, 128, 512)


_NC = None


def kernel(x, norm_ffn1, norm_mix, norm_ffn2, norm_out, ffn_w_gate, ffn_w_up, ffn_w_down,
           even_w_in, pool_w, pool_scale, dil_q_gain, dil_k_gain, even_w_out,
           odd_w_in, na_q_gain, na_k_gain, na_rpb, sg_v_gain, sg_w, sg_b, odd_w_out):
    global _NC
    f = lambda a: np.ascontiguousarray(np.asarray(a, dtype=np.float32))
    x = f(x)
    norms = [f(norm_ffn1), f(norm_mix), f(norm_ffn2), f(norm_out)]
    gains = np.zeros((128, 128), np.float32)
    for l in range(2):
        for k in range(4):
            gains[:, (l * 4 + k) * 16:(l * 4 + k + 1) * 16] = norms[k][l].reshape(16, 128).T
    hg = np.stack([f(dil_q_gain)[0], f(dil_k_gain)[0], f(na_q_gain)[0], f(na_k_gain)[0]], axis=1)
    pscale = np.ascontiguousarray(f(pool_scale)[0].reshape(4, 128).T)
    shared = {
        "gains": gains, "hg": np.ascontiguousarray(hg), "pscale": pscale,
        "wg": f(ffn_w_gate).reshape(4, D, DFF), "wu": f(ffn_w_up).reshape(4, D, DFF), "wd": f(ffn_w_down).reshape(4, DFF, D),
        "ewin": f(even_w_in)[0], "ewout": f(even_w_out)[0], "owin": f(odd_w_in)[0], "owout": f(odd_w_out)[0],
        "poolw": f(pool_w)[0], "sgw": np.ascontiguousarray(f(sg_w)[0].transpose(0, 2, 1)),
        "sgb": np.ascontiguousarray(np.broadcast_to(f(sg_b)[0].reshape(1, 1024), (128, 1024))),
        "sgvg": np.ascontiguousarray(np.broadcast_to(f(sg_v_gain)[0].reshape(1, 1024), (128, 1024))),
        "etab": _etab(),
    }
    rpb = f(na_rpb)[0]
    in_maps = []
    for c in range(8):
        b, cp = c // 4, c % 4
        s0 = cp * NT
        m = dict(shared)
        m["xT"] = np.ascontiguousarray(x[b, s0:s0 + NT, :].T)
        vc = np.ones((128, 4), np.float32)
        if cp == 0:
            vc[0:64, 1] = 0.0
        if cp == 3:
            vc[64:128, 2] = 0.0
            vc[:, 3] = 0.0
        m["vcols"] = vc
        pv = np.ones((128, 2), np.float32)
        if cp == 0:
            pv[:, 0] = 0.0
        if cp == 3:
            pv[:, 1] = 0.0
        m["pvalid"] = pv
        pf = np.ones((4, 16), np.float32)
        for g, w in enumerate((2, 4, 8, 16)):
            for j in range(16):
                t = s0 + (j if j < 8 else NT - 16 + j)
                lo = min(max(t - w // 2, 0), 4096)
                hi = min(max(t + w // 2, 0), 4096)
                pf[g, j] = w / float(hi - lo)
        m["poolfix"] = np.ascontiguousarray(np.broadcast_to(pf.reshape(1, 64), (128, 64)))
        m["natab"] = _natab(rpb, s0)
        pr = c - 1 if cp != 0 else c
        nx = c + 1 if cp != 3 else c
        m["info"] = np.array([[pr * NSE, nx * NSE, pr * NSO, nx * NSO]], np.int32)
        in_maps.append(m)
    if _NC is None:
        _NC = build()
    res = run_bass_kernel_spmd(_NC, in_maps, core_ids=list(range(8)))
    out = np.empty((2, 4096, D), np.float32)
    for c in range(8):
        b, cp = c // 4, c % 4
        out[b, cp * NT:(cp + 1) * NT, :] = np.asarray(res.results[c]["outT"]).T
    return out
```
